# Optimizing a Trainium2 kernel written in Bass

```python
import math
import jax, jax.numpy as jnp
from jax import lax
import numpy as np

D_MODEL = 2048
BATCH = 8
SEQ = 2048
DEPTH = 1

HG_HEADS = 8
HG_DK = 128
HG_DV = 128
HG_WIDTH = HG_HEADS * HG_DK
HG_CHUNK = 64

AT_HEADS = 16
AT_KV_HEADS = 4
AT_HEAD_DIM = 64
AT_GROUP = AT_HEADS // AT_KV_HEADS
AT_WIDTH = AT_HEADS * AT_HEAD_DIM
KV_WIDTH = AT_KV_HEADS * AT_HEAD_DIM
WINDOW = 128
BLOCK = 128

N_BUCKETS = 32
MAX_EXACT = N_BUCKETS // 2
MAX_DISTANCE = 128

D_FF = 4 * D_MODEL
N_BRANCH = 2
EPS = 1e-6
NEG_INF = -1e30

IN_WIDTH = 4 * HG_WIDTH + AT_WIDTH + 2 * KV_WIDTH + N_BRANCH * D_MODEL
IN_OFFSETS = (
    HG_WIDTH,
    2 * HG_WIDTH,
    3 * HG_WIDTH,
    4 * HG_WIDTH,
    4 * HG_WIDTH + AT_WIDTH,
    4 * HG_WIDTH + AT_WIDTH + KV_WIDTH,
    4 * HG_WIDTH + AT_WIDTH + 2 * KV_WIDTH,
    4 * HG_WIDTH + AT_WIDTH + 2 * KV_WIDTH + D_MODEL,
)

kernel_name = "hgrn2_swa_sink_gated_hybrid_block"


def rms_norm(x, g):
    xf = x.astype(jnp.float32)
    y = xf * lax.rsqrt(jnp.mean(xf * xf, axis=-1, keepdims=True) + EPS)
    return (y * g.astype(jnp.float32)).astype(x.dtype)


def modulate(h, shift, scale):
    return h * (1.0 + scale[:, None, :]) + shift[:, None, :]


def t5_causal_bucket(n):
    nf = jnp.maximum(n, 1).astype(jnp.float32)
    large = MAX_EXACT + (jnp.log(nf / MAX_EXACT) / math.log(MAX_DISTANCE / MAX_EXACT)
                         * (N_BUCKETS - MAX_EXACT)).astype(jnp.int32)
    large = jnp.minimum(large, N_BUCKETS - 1)
    return jnp.where(n < MAX_EXACT, n, large)


def band_geometry(n_blocks):
    i = jnp.arange(BLOCK, dtype=jnp.int32)[:, None]
    j = jnp.arange(2 * BLOCK, dtype=jnp.int32)[None, :]
    dist = i - j + BLOCK
    blk = jnp.arange(n_blocks, dtype=jnp.int32)[:, None, None]
    key_pos = blk * BLOCK - BLOCK + j
    mask = (dist >= 0) & (dist < WINDOW) & (key_pos >= 0)
    bucket = t5_causal_bucket(jnp.maximum(dist, 0))
    return mask, bucket


def hgrn2_chunkwise(q, log_f, k, v):
    B, H, L, DK = q.shape
    DV = v.shape[-1]
    C = HG_CHUNK
    N = L // C
    q = q.reshape(B, H, N, C, DK)
    k = k.reshape(B, H, N, C, DK)
    v = v.reshape(B, H, N, C, DV)
    b = jnp.cumsum(log_f.reshape(B, H, N, C, DK), axis=3)
    ref = b[:, :, :, C // 2 - 1:C // 2]
    b_last = b[:, :, :, C - 1:]
    scores = jnp.einsum('bhncd,bhnsd->bhncs', q * jnp.exp(b - ref), k * jnp.exp(ref - b))
    causal = jnp.tril(jnp.ones((C, C), dtype=bool))
    scores = jnp.where(causal, scores, 0.0)
    o = jnp.einsum('bhncs,bhnsv->bhncv', scores, v)
    upd = jnp.einsum('bhncd,bhncv->bhndv', k * jnp.exp(b_last - b), v)
    decay = jnp.exp(b_last[:, :, :, 0])

    def step(S, xs):
        dec, u = xs
        return dec[..., None] * S + u, S

    _, S_prev = lax.scan(step, jnp.zeros((B, H, DK, DV), q.dtype),
                         (jnp.moveaxis(decay, 2, 0), jnp.moveaxis(upd, 2, 0)))
    S_prev = jnp.moveaxis(S_prev, 0, 2)
    o = o + jnp.einsum('bhncd,bhndv->bhncv', q * jnp.exp(b), S_prev)
    return o.reshape(B, H, L, DV)


def sink_swa(q, k, v, sinks, rel_bias_table):
    B, L = q.shape[0], q.shape[1]
    nb = L // BLOCK
    qb = q.reshape(B, nb, BLOCK, AT_KV_HEADS, AT_GROUP, AT_HEAD_DIM)

    def band(t):
        tb = t.reshape(B, nb, BLOCK, AT_KV_HEADS, AT_HEAD_DIM)
        prev = jnp.pad(tb, ((0, 0), (1, 0), (0, 0), (0, 0), (0, 0)))[:, :-1]
        return jnp.concatenate([prev, tb], axis=2)

    kk, vv = band(k), band(v)
    mask, bucket = band_geometry(nb)
    bias = jnp.transpose(rel_bias_table[bucket], (2, 0, 1)).astype(jnp.float32)
    bias = bias.reshape(AT_KV_HEADS, AT_GROUP, BLOCK, 2 * BLOCK)
    scale = AT_HEAD_DIM ** -0.5
    logits = jnp.einsum('bnqkgd,bnskd->bnkgqs', qb, kk).astype(jnp.float32) * scale + bias
    logits = jnp.where(mask[None, :, None, None], logits, NEG_INF)
    sink = jnp.broadcast_to(sinks.astype(jnp.float32).reshape(AT_KV_HEADS, AT_GROUP, 1, 1),
                            logits.shape[:-1] + (1,))
    p = jax.nn.softmax(jnp.concatenate([logits, sink], axis=-1), axis=-1)[..., :2 * BLOCK]
    o = jnp.einsum('bnkgqs,bnskd->bnqkgd', p.astype(vv.dtype), vv)
    return o.reshape(B, L, AT_WIDTH)


def setup_inputs(seed: int = 0) -> dict:
    key = jax.random.key(seed)
    ks = jax.random.split(key, 20)

    def nrm(k, shape, s):
        return jax.random.normal(k, shape, jnp.float32) * s

    return {
        "x": nrm(ks[0], (BATCH, SEQ, D_MODEL), 1.0),
        "c": nrm(ks[1], (BATCH, D_MODEL), 1.0),
        "w_ada": nrm(ks[2], (DEPTH, D_MODEL, 6 * D_MODEL), 0.5 * D_MODEL ** -0.5),
        "b_ada": nrm(ks[3], (DEPTH, 6 * D_MODEL), 0.02),
        "norm1_g": 1.0 + nrm(ks[4], (DEPTH, D_MODEL), 0.02),
        "norm2_g": 1.0 + nrm(ks[5], (DEPTH, D_MODEL), 0.02),
        "w_in": nrm(ks[6], (DEPTH, D_MODEL, IN_WIDTH), D_MODEL ** -0.5),
        "hg_lb_logits": nrm(ks[7], (DEPTH + 1, HG_WIDTH), 1.0),
        "hg_out_norm_g": 1.0 + nrm(ks[8], (DEPTH, HG_DV), 0.02),
        "q_norm_g": 1.0 + nrm(ks[9], (DEPTH, AT_HEAD_DIM), 0.02),
        "k_norm_g": 1.0 + nrm(ks[10], (DEPTH, AT_HEAD_DIM), 0.02),
        "attn_sinks": nrm(ks[11], (DEPTH, AT_HEADS), 1.0),
        "rel_bias_table": nrm(ks[12], (N_BUCKETS, AT_HEADS), 0.5),
        "w_branch_hg": nrm(ks[13], (DEPTH, HG_WIDTH, D_MODEL), HG_WIDTH ** -0.5),
        "w_branch_attn": nrm(ks[14], (DEPTH, AT_WIDTH, D_MODEL), AT_WIDTH ** -0.5),
        "w_out": nrm(ks[15], (DEPTH, D_MODEL, D_MODEL), D_MODEL ** -0.5),
        "w_ff1": nrm(ks[16], (DEPTH, D_MODEL, D_FF), D_MODEL ** -0.5),
        "w_ff2": nrm(ks[17], (DEPTH, D_FF, D_MODEL), D_FF ** -0.5),
    }


def reference(x, c, w_ada, b_ada, norm1_g, norm2_g, w_in, hg_lb_logits, hg_out_norm_g,
              q_norm_g, k_norm_g, attn_sinks, rel_bias_table, w_branch_hg, w_branch_attn,
              w_out, w_ff1, w_ff2):
    B, L, _ = x.shape
    lb_all = jnp.cumsum(jax.nn.softmax(hg_lb_logits.astype(jnp.float32), axis=0), axis=0)
    c_act = jax.nn.silu(c)
    for l in range(DEPTH):
        ada = c_act @ w_ada[l] + b_ada[l]
        shift1, scale1, gate1, shift2, scale2, gate2 = jnp.split(ada, 6, axis=-1)

        h = modulate(rms_norm(x, norm1_g[l]), shift1, scale1)
        proj = h @ w_in[l]
        hq, hf, hi, hg, aq, ak, av, gate_hg, gate_at = jnp.split(proj, IN_OFFSETS, axis=-1)

        lb = lb_all[l]
        f = lb + (1.0 - lb) * jax.nn.sigmoid(hf.astype(jnp.float32))
        log_f = jnp.log(f)

        def to_heads(t):
            return t.reshape(B, L, HG_HEADS, HG_DK).transpose(0, 2, 1, 3)

        o_hg = hgrn2_chunkwise(to_heads(jax.nn.silu(hq.astype(jnp.float32))), to_heads(log_f),
                               to_heads(1.0 - f), to_heads(hi.astype(jnp.float32)))
        o_hg = o_hg.transpose(0, 2, 1, 3).astype(x.dtype)
        o_hg = rms_norm(o_hg, hg_out_norm_g[l]) * jax.nn.silu(hg.reshape(B, L, HG_HEADS, HG_DV))
        o_hg = o_hg.reshape(B, L, HG_WIDTH)

        q = rms_norm(aq.reshape(B, L, AT_HEADS, AT_HEAD_DIM), q_norm_g[l])
        k = rms_norm(ak.reshape(B, L, AT_KV_HEADS, AT_HEAD_DIM), k_norm_g[l])
        v = av.reshape(B, L, AT_KV_HEADS, AT_HEAD_DIM)
        o_at = sink_swa(q, k, v, attn_sinks[l], rel_bias_table)

        merged = (jax.nn.sigmoid(gate_hg) * (o_hg @ w_branch_hg[l])
                  + jax.nn.sigmoid(gate_at) * (o_at @ w_branch_attn[l]))
        x = x + gate1[:, None, :] * (merged @ w_out[l])

        h2 = modulate(rms_norm(x, norm2_g[l]), shift2, scale2)
        ff = jnp.square(jax.nn.relu(h2 @ w_ff1[l])) @ w_ff2[l]
        x = x + gate2[:, None, :] * ff
    return x
```

```python
from contextlib import ExitStack
import math
import numpy as np
import concourse.bass as bass
import concourse.mybir as mybir
from concourse.bass_utils import run_bass_kernel_spmd

F32 = mybir.dt.float32
BF16 = mybir.dt.bfloat16
ALU = mybir.AluOpType
AF = mybir.ActivationFunctionType
EPS = 1e-6
NEG = -30000.0


class StopBuild(Exception):
    pass


class Op:
    __slots__ = ("eng", "fn", "deps", "signaled", "seq", "is_dma", "dma_slot", "dma_val", "phase", "slotmax")


class Sched:
    ENG = ("pe", "act", "dve", "pool", "sp")

    def __init__(self, nc, es):
        self.nc = nc
        self.last_w = {}
        self.readers = {}
        self.dma_cnt = {}
        self.last_dma = {}
        self.dsems = {}
        self.es = es
        self.sems = {e: es.enter_context(nc.semaphore(f"s_{e}")) for e in self.ENG}
        self.seqc = {e: 0 for e in self.ENG}
        self.phase = 0
        self.streams = {e: [] for e in self.ENG}
        self.nops = 0
        self.stop_after = None
        self.op_limit = None
        self.oplog = []

    def add(self, eng, fn, reads=(), writes=(), dma_slot=None, extra_deps=(), chain=True):
        if self.op_limit is not None and self.nops >= self.op_limit:
            op = Op()
            op.eng, op.is_dma, op.phase, op.signaled, op.dma_slot = eng, False, -1, False, None
            return op
        if self.op_limit is not None:
            import sys as _sys
            f = _sys._getframe(1)
            while f is not None and f.f_code.co_name != "build":
                f = f.f_back
            self.oplog.append((self.nops, eng, f.f_lineno if f else -1))
        op = Op()
        op.eng, op.fn, op.signaled, op.seq = eng, fn, False, 0
        op.is_dma = dma_slot is not None
        op.dma_slot, op.dma_val, op.phase = dma_slot, 0, self.phase
        deps, seen = [], set()

        def add_dep(d):
            if d is None or id(d) in seen:
                return
            seen.add(id(d))
            if not d.is_dma:
                if d.phase != self.phase:
                    return
                if (not op.is_dma) and d.eng == "pe" and eng == "pe":
                    return
            deps.append(d)

        for r in reads:
            add_dep(self.last_w.get(r))
        for r in writes:
            add_dep(self.last_w.get(r))
            for rd in self.readers.get(r, {}).values():
                add_dep(rd)
        for d in extra_deps:
            add_dep(d)
        if op.is_dma:
            if chain:
                add_dep(self.last_dma.get(dma_slot))
            self.last_dma[dma_slot] = op
        op.deps = deps
        op.slotmax = {d.dma_slot: 16 * self.dma_cnt[d.dma_slot] for d in deps if d.is_dma}
        for d in deps:
            if not d.is_dma:
                d.signaled = True
        for r in reads:
            key = ("dma", dma_slot) if op.is_dma else eng
            self.readers.setdefault(r, {})[key] = op
        for r in writes:
            self.last_w[r] = op
            self.readers[r] = {}
        if op.is_dma:
            if dma_slot not in self.dsems:
                self.dsems[dma_slot] = self.es.enter_context(self.nc.semaphore(f"d_{dma_slot}"))
            c = self.dma_cnt.get(dma_slot, 0) + 1
            self.dma_cnt[dma_slot] = c
            op.dma_val = 16 * c
        self.streams[eng].append(op)
        self.nops += 1
        return op

    def emit_phase(self, final_waits=()):
        nc = self.nc
        lastd = {}
        for e in self.ENG:
            for op in self.streams[e]:
                if op.is_dma:
                    lastd[op.dma_slot] = op
        fw = list(final_waits) + list(lastd.values())
        if fw:
            self.add("sp", None, extra_deps=fw)
        for e in self.ENG:
            k = self.seqc[e]
            for op in self.streams[e]:
                if op.signaled and not op.is_dma:
                    k += 1
                    op.seq = k
            self.seqc[e] = k
        streams = self.streams
        sems, dsems = self.sems, self.dsems

        def run_stream(e, eng):
            waited = {}
            for op in streams[e]:
                for d in op.deps:
                    if d.is_dma:
                        key, val, sem = ("d", d.dma_slot), max(d.dma_val, op.slotmax.get(d.dma_slot, 0)), dsems[d.dma_slot]
                    else:
                        key, val, sem = ("e", d.eng), d.seq, sems[d.eng]
                    if waited.get(key, 0) >= val:
                        continue
                    waited[key] = val
                    eng.wait_ge(sem, val)
                ins = op.fn(eng) if op.fn is not None else None
                if op.is_dma:
                    ins.then_inc(dsems[op.dma_slot], 16)
                elif op.signaled:
                    ins.then_inc(sems[e], 1)

        with nc.Block() as block:
            if streams["pe"]:
                @block.tensor
                def _(eng):
                    run_stream("pe", eng)
            if streams["act"]:
                @block.scalar
                def _(eng):
                    run_stream("act", eng)
            if streams["dve"]:
                @block.vector
                def _(eng):
                    run_stream("dve", eng)
            if streams["pool"]:
                @block.gpsimd
                def _(eng):
                    run_stream("pool", eng)
            if streams["sp"]:
                @block.sync
                def _(eng):
                    run_stream("sp", eng)
        self.streams = {e: [] for e in self.ENG}
        self.phase += 1
        if self.stop_after is not None and self.phase > self.stop_after:
            raise StopBuild()


class Cfg:
    def __init__(self, D=2048, L=2048, LS=1024, TT=512, HH=8, AH=16, KVH=4, DFF=8192, debug=False):
        self.D, self.L, self.LS, self.TT, self.HH, self.AH, self.KVH, self.DFF = D, L, LS, TT, HH, AH, KVH, DFF
        self.KC = D // 128
        self.HGW = HH * 128
        self.AQW = AH * 64
        self.AQC = self.AQW // 128
        self.KVW = KVH * 64
        self.FC = DFF // 128
        self.INW = 4 * self.HGW + self.AQW + 2 * self.KVW + 2 * D
        self.NSEG = L // LS
        self.NT = LS // TT
        self.NSUB = TT // 128
        self.SS = LS // 128
        self.debug = debug
        assert AH // KVH == 4 and self.HH + self.AQC <= self.KC and DFF % (self.KC * 128) == 0
        off = {}
        o = 0
        for name, n in (("c", self.KC), ("bada", 6 * self.KC), ("n1g", self.KC), ("n2g", self.KC), ("lb0", HH),
                        ("lb1", HH), ("hgo", 1), ("gq", 1), ("gk", 1), ("sink", AH // 2), ("ident", 128),
                        ("hgmask", 128), ("bd", 128)):
            off[name] = (o, o + n)
            o += n
        self.ppoff, self.NP = off, o


def t5_bucket(n):
    nb, mx, md = 32, 16, 128
    if n < mx:
        return n
    v = np.float32(np.log(np.float32(n) / np.float32(mx))) / np.float32(math.log(md / mx)) * np.float32(nb - mx)
    return min(int(mx + np.int32(v)), nb - 1)


def host_consts(cfg):
    ohg = np.zeros((33, 512), np.float32)
    for v in range(256):
        if v < 128:
            ohg[t5_bucket(v), v] = 1.0
        else:
            ohg[32, v] = NEG
        if v > 128:
            ohg[t5_bucket(v - 128), 256 + v] = 1.0
        else:
            ohg[32, 256 + v] = NEG
    ident = np.eye(128, dtype=np.float32)
    s = np.arange(128)[:, None]
    c = np.arange(128)[None, :]
    hgmask = ((s // 64 == c // 64) & (s <= c)).astype(np.float32)
    bd = (s // 64 == c // 64).astype(np.float32)
    return ohg, ident, hgmask, bd


def build(cfg):
    nc = bass.Bass("TRN2", target_bir_lowering=False)
    D, L, LS, TT, KC, HH, AH, KVH, DFF, FC = cfg.D, cfg.L, cfg.LS, cfg.TT, cfg.KC, cfg.HH, cfg.AH, cfg.KVH, cfg.DFF, cfg.FC
    HGW, AQW, AQC, KVW, NT, NSUB, SS = cfg.HGW, cfg.AQW, cfg.AQC, cfg.KVW, cfg.NT, cfg.NSUB, cfg.SS
    NPAIR = TT // 128
    NCH = TT // 64

    def din(name, shape):
        return nc.dram_tensor(name, list(shape), F32, kind="ExternalInput").ap()

    x_d = din("x", [L, D])
    pp_d = din("pp", [128, cfg.NP])
    tab_d = din("tabaug", [33, AH])
    ohg_d = din("ohg", [33, 512])
    wada_d = din("w_ada", [D, 6 * D])
    win_d = din("w_in", [D, cfg.INW])
    wbh_d = din("w_bh", [HGW, D])
    wba_d = din("w_ba", [AQW, D])
    wout_d = din("w_out", [D, D])
    wff1_d = din("w_ff1", [D, DFF])
    wff2_d = din("w_ff2", [DFF, D])
    out_d = nc.dram_tensor("out", [L, D], F32, kind="ExternalOutput").ap()
    x1_d = nc.dram_tensor("x1s", [L, D], F32).ap()
    z_d = nc.dram_tensor("zbias", [AH, 2, 130, 256], F32).ap()
    grow_d = nc.dram_tensor("growd", [2, 128, D], F32).ap()
    wff1c_d = nc.dram_tensor("wff1c", [D, DFF], BF16).ap()
    wff2c_d = nc.dram_tensor("wff2c", [DFF, D], BF16).ap()
    dbg = {}
    if cfg.debug:
        dbg["hT"] = nc.dram_tensor("dbg_hT", [128, KC, L], BF16, kind="ExternalOutput").ap()
        dbg["ohg"] = nc.dram_tensor("dbg_ohg", [128, HH, L], BF16, kind="ExternalOutput").ap()
        dbg["oat"] = nc.dram_tensor("dbg_oat", [128, AQC, L], BF16, kind="ExternalOutput").ap()
        dbg["mg"] = nc.dram_tensor("dbg_mg", [128, KC, L], BF16, kind="ExternalOutput").ap()
        dbg["ada"] = nc.dram_tensor("dbg_ada", [128, 6 * KC], F32, kind="ExternalOutput").ap()
        dbg["x1"] = nc.dram_tensor("dbg_x1", [L, D], F32, kind="ExternalOutput").ap()

    HQ0, HF0, HI0, HG0 = 0, HGW, 2 * HGW, 3 * HGW
    AQ0 = 4 * HGW
    AK0 = AQ0 + AQW
    AV0 = AK0 + KVW
    GH0 = AV0 + KVW
    GA0 = GH0 + D

    try:
      with ExitStack() as es:
        S = Sched(nc, es)
        S.stop_after = getattr(cfg, "stop_after", None)
        S.op_limit = getattr(cfg, "op_limit", None)
        cfg._sched = S

        sbn = [0]

        def sb(stack, name, shape, dt):
            sbn[0] += 1
            return stack.enter_context(nc.sbuf_tensor(f"sb{sbn[0]}_{name}", list(shape), dt))

        RW = KC * 512
        ring = []
        ring_i = [0]

        def set_ring(stack, n, width=512):
            ring[:] = [sb(stack, f"ring{i}", [128, KC * width], BF16) for i in range(n)]
            ring_i[0] = 0
        pp = sb(es, "pp", [128, cfg.NP], F32)
        identb = sb(es, "identb", [128, 128], BF16)
        hgmask = sb(es, "hgmaskb", [128, 128], BF16)
        bdb = sb(es, "bdb", [128, 128], BF16)
        onesb = sb(es, "onesb", [128, 128], BF16)
        onesf = sb(es, "onesf", [128, 128], F32)
        ones64 = sb(es, "ones64", [128, 64], F32)
        identf = sb(es, "identf", [128, 128], F32)
        cact = sb(es, "cact", [128, KC], BF16)
        ada = sb(es, "ada", [128, 6 * KC], F32)
        m1s = sb(es, "m1s", [128, KC], F32)
        m2s = sb(es, "m2s", [128, KC], F32)
        lb = sb(es, "lb", [128, HH], F32)
        gq = sb(es, "gq", [128, 1], F32)
        esel = sb(es, "esel", [128, AH // 2], F32)
        hT = sb(es, "hT", [128, KC, LS], BF16)
        Sst = [sb(es, f"Sst{i}", [128, HH, 128], F32) for i in range(2)]
        kcar = sb(es, "kcar", [128, KVH, 128], BF16)
        vcar = sb(es, "vcar", [128, KVH, 64], BF16)
        psf, psb = [], []
        psf_i = [0]
        psb_i = [0]
        psn = [0]
        pools = {}

        def set_psum(stack, nf, nb):
            psn[0] += 1
            psf[:] = [stack.enter_context(nc.psum_tensor(f"psf{psn[0]}_{i}", [128, 512], F32)) for i in range(nf)]
            psb[:] = [stack.enter_context(nc.psum_tensor(f"psb{psn[0]}_{i}", [128, 1024], BF16)) for i in range(nb)]
            psf_i[0] = 0
            psb_i[0] = 0
            pools.clear()

        def bankp(name, idxs):
            c = pools.get(name, 0)
            pools[name] = c + 1
            i = idxs[c % len(idxs)]
            return psf[i], ("psf", i)

        def bank():
            i = psf_i[0] % len(psf)
            psf_i[0] += 1
            return psf[i], ("psf", i)

        def bankb():
            i = psb_i[0] % len(psb)
            psb_i[0] += 1
            return psb[i], ("psb", i)

        def ppc(name):
            a, b = cfg.ppoff[name]
            return pp[:, a:b]

        def act(out, in_, func, reads, writes, bias=None, scale=None, accum=None):
            kw = {}
            if bias is not None:
                kw["bias"] = bias
            if scale is not None:
                kw["scale"] = scale
            if accum is not None:
                kw["accum_out"] = accum
            return S.add("act", lambda e: e.activation(out=out, in_=in_, func=func, **kw), reads=reads, writes=writes)

        def tt(eng, out, in0, in1, op, reads, writes):
            return S.add(eng, lambda e: e.tensor_tensor(out=out, in0=in0, in1=in1, op=op), reads=reads, writes=writes)

        def ts(eng, out, in0, s1, s2, op0, op1, reads, writes):
            if s2 is None:
                return S.add(eng, lambda e: e.tensor_scalar(out=out, in0=in0, scalar1=s1, scalar2=None, op0=op0),
                             reads=reads, writes=writes)
            return S.add(eng, lambda e: e.tensor_scalar(out=out, in0=in0, scalar1=s1, scalar2=s2, op0=op0, op1=op1),
                         reads=reads, writes=writes)

        def stt(out, in0, scalar, in1, op0, op1, reads, writes):
            return S.add("dve", lambda e: e.scalar_tensor_tensor(out=out, in0=in0, scalar=scalar, in1=in1, op0=op0, op1=op1),
                         reads=reads, writes=writes)

        def cp(eng, out, in_, reads, writes):
            if eng == "act":
                return S.add("act", lambda e: e.copy(out=out, in_=in_), reads=reads, writes=writes)
            return S.add(eng, lambda e: e.tensor_copy(out=out, in_=in_), reads=reads, writes=writes)

        def dma(q, out, in_, reads, writes, slot, chain=True):
            return S.add(q, lambda e: e.dma_start(out=out, in_=in_), reads=reads, writes=writes, dma_slot=slot, chain=chain)

        def mmgroup(out, pairs, reads, writes):
            n = len(pairs)

            def fn(e):
                ins = None
                for i, (l, r) in enumerate(pairs):
                    ins = e.matmul(out, lhsT=l, rhs=r, start=(i == 0), stop=(i == n - 1))
                return ins
            return S.add("pe", fn, reads=reads, writes=writes)

        def wload(w_ap, row0, nk, col0, ncols, coloff=0, slot=None, kcoff=0, kstride=None, chain=True):
            if slot is None:
                slot = ring_i[0] % len(ring)
                ring_i[0] += 1
            ks = kstride if kstride is not None else ncols
            view = ring[slot][:, 0:KC * ks].rearrange("p (k n) -> p k n", k=KC)[:, kcoff:kcoff + nk, coloff:coloff + ncols]
            src = w_ap[row0:row0 + nk * 128, col0:col0 + ncols].rearrange("(k p) n -> p k n", p=128)
            if chain:
                dma("pool", view, src, [], [("ring", slot)], f"ring{slot}")
            else:
                op = S.add("pool", lambda e: e.dma_start(out=view, in_=src), dma_slot=f"ring{slot}", chain=False)
                S.last_w[("ring", slot)] = op
            return slot

        cast_jobs = []
        for fct in range(DFF // 512):
            cast_jobs.append((wff1_d, wff1c_d, 0, D, fct * 512, ("wc1", fct)))
        for kb in range(DFF // (KC * 128)):
            for pn in range(D // 512):
                cast_jobs.append((wff2_d, wff2c_d, kb * KC * 128, KC * 128, pn * 512, ("wc2", kb, pn)))
        cast_n = [0]
        NCAST = len(cast_jobs)

        def issue_casts(n):
            for _ in range(n):
                if cast_n[0] >= NCAST:
                    return
                src_t, dst_t, r0, nr, c0, key = cast_jobs[cast_n[0]]
                i = cast_n[0]
                cast_n[0] += 1
                dma("pool", dst_t[r0:r0 + nr, c0:c0 + 512], src_t[r0:r0 + nr, c0:c0 + 512], [], [key], f"cast{i}")

        def wload_c(c_ap, row0, nk, col0, ncols, key):
            slot = ring_i[0] % len(ring)
            ring_i[0] += 1
            view = ring[slot][:, 0:KC * ncols].rearrange("p (k n) -> p k n", k=KC)[:, 0:nk, :]
            src = c_ap[row0:row0 + nk * 128, col0:col0 + ncols].rearrange("(k p) n -> p k n", p=128)
            dma("pool", view, src, [key], [("ring", slot)], f"ring{slot}")
            return slot

        def rview(slot, ks):
            return ring[slot][:, 0:KC * ks].rearrange("p (k n) -> p k n", k=KC)

        def rmsnorm_gen(stack_tmps, par, src_tile, src_key, dstT, col0, msc, msh, dkey, modkey=("mod",)):
            junk, ssq, lnv, rstd, xn = stack_tmps[par]
            act(junk[:], src_tile, AF.Square, [src_key], [("junk", par), ("ssq", par)], accum=ssq[:])
            act(lnv[:], ssq[:], AF.Ln, [("ssq", par)], [("lnv", par)], bias=EPS, scale=1.0 / D)
            act(rstd[:], lnv[:], AF.Exp, [("lnv", par)], [("rstd", par)], scale=-0.5)
            ts("dve", xn[:], src_tile, rstd[:, 0:1], None, ALU.mult, None, [src_key, ("rstd", par)], [("xn", par)])
            yield
            for g in range((KC + 7) // 8):
                n8 = min(8, KC - 8 * g)
                pb, pk = bankb()

                def tr(e, g=g, n8=n8, pb=pb):
                    ins = None
                    for j in range(n8):
                        kc = 8 * g + j
                        ins = e.transpose(out=pb[:, j * 128:(j + 1) * 128], in_=xn[:, kc * 128:(kc + 1) * 128], identity=identb[:])
                    return ins
                S.add("pe", tr, reads=[("xn", par), ("identb",)], writes=[pk])
                for j in range(n8):
                    kc = 8 * g + j
                    if j % 2 == 0:
                        act(dstT[:, kc, col0:col0 + 128], pb[:, j * 128:(j + 1) * 128], AF.Identity, [pk, modkey], [dkey],
                            bias=msh[:, kc:kc + 1], scale=msc[:, kc:kc + 1])
                    else:
                        ts("dve", dstT[:, kc, col0:col0 + 128], pb[:, j * 128:(j + 1) * 128], msc[:, kc:kc + 1], msh[:, kc:kc + 1],
                           ALU.mult, ALU.add, [pk, modkey], [dkey])
                yield

        def norm_tmps(stack):
            return [(sb(stack, f"junk{i}", [128, D], BF16), sb(stack, f"ssq{i}", [128, 1], F32), sb(stack, f"lnv{i}", [128, 1], F32),
                     sb(stack, f"rstd{i}", [128, 1], F32), sb(stack, f"xn{i}", [128, D], BF16)) for i in range(2)]

        def interleave(ga, gb, nb_per_a=1):
            alive_a, alive_b = ga is not None, gb is not None
            while alive_a or alive_b:
                if alive_a:
                    try:
                        next(ga)
                    except StopIteration:
                        alive_a = False
                for _ in range(nb_per_a):
                    if alive_b:
                        try:
                            next(gb)
                        except StopIteration:
                            alive_b = False

        with ExitStack() as ph:
            set_ring(ph, 4)
            set_psum(ph, 6, 2)
            tabs = sb(ph, "tabs", [33, AH], F32)
            ohgs = sb(ph, "ohgs", [33, 512], F32)
            t0 = sb(ph, "t0", [128, 8 * KC], F32)
            t1 = sb(ph, "t1", [128, 8 * KC], F32)
            gsb = sb(ph, "gsb", [AH, 512], F32)

            dma("sp", pp[:], pp_d[:, :], [], [("pp",)], "c0")
            dma("sp", tabs[:], tab_d[:, :], [], [("tabs",)], "c0")
            dma("sp", ohgs[:], ohg_d[:, :], [], [("ohgs",)], "c0")
            cp("dve", identb[:], ppc("ident"), [("pp",)], [("identb",)])
            cp("dve", identf[:], ppc("ident"), [("pp",)], [("identf",)])
            cp("dve", hgmask[:], ppc("hgmask"), [("pp",)], [("hgmask",)])
            cp("dve", bdb[:], ppc("bd"), [("pp",)], [("bdb",)])
            S.add("pool", lambda e: e.memset(onesb[:], 1.0), writes=[("onesb",)])
            S.add("pool", lambda e: e.memset(onesf[:], 1.0), writes=[("onesf",)])
            S.add("pool", lambda e: e.memset(ones64[:], 1.0), writes=[("ones64",)])
            for i in range(2):
                S.add("pool", lambda e, i=i: e.memset(Sst[i][:], 0.0), writes=[("Sst", i)])
            act(t0[:, 0:KC], ppc("c"), AF.Exp, [("pp",)], [("t0",)], scale=-1.0)
            ts("dve", t0[:, 0:KC], t0[:, 0:KC], 1.0, None, ALU.add, None, [("t0",)], [("t0",)])
            S.add("dve", lambda e: e.reciprocal(out=t1[:, 0:KC], in_=t0[:, 0:KC]), reads=[("t0",)], writes=[("t1",)])
            tt("dve", cact[:], t1[:, 0:KC], ppc("c"), ALU.mult, [("t1",), ("pp",)], [("cact",)])
            tt("dve", t0[:, 0:HH], ppc("lb1"), ppc("lb0"), ALU.subtract, [("pp",), ("t1",)], [("t0",)])
            act(t0[:, 0:HH], t0[:, 0:HH], AF.Exp, [("t0",)], [("t0",)])
            ts("dve", t0[:, 0:HH], t0[:, 0:HH], 1.0, None, ALU.add, None, [("t0",)], [("t0",)])
            S.add("dve", lambda e: e.reciprocal(out=lb[:], in_=t0[:, 0:HH]), reads=[("t0",)], writes=[("lb",)])
            ts("dve", gq[:], ppc("gq"), 0.125, None, ALU.mult, None, [("pp",)], [("gq",)])
            act(esel[:], ppc("sink"), AF.Exp, [("pp",)], [("esel",)])
            gps, gk_ = bank()
            mmgroup(gps[0:AH, :], [(tabs[:], ohgs[:])], [("tabs",), ("ohgs",)], [gk_])
            cp("dve", gsb[:], gps[0:AH, :], [gk_], [("gsb",)])
            zops = []
            for cpv in range(2):
                src = gsb[:, cpv * 256:(cpv + 1) * 256].unsqueeze(1).to_broadcast([AH, 130, 256])
                zops.append(dma("sp", z_d[:, cpv, :, :], src, [("gsb",)], [("z", cpv)], "z"))
            def ada_tiles(t0_, t1_, adaps, adak):
                for t in range(t0_, t1_):
                    slot = wload(wada_d, 0, KC, t * 512, 512)
                    wv = rview(slot, 512)
                    for j in range(4):
                        col = t * 4 + j
                        mmgroup(adaps[:, col:col + 1], [(wv[:, kc, j * 128:(j + 1) * 128], cact[:, kc:kc + 1]) for kc in range(KC)],
                                [("ring", slot), ("cact",)], [adak])
            ntile = 6 * D // 512
            nt0 = 2 * D // 512
            adaps, adak = bank()
            ada_tiles(0, nt0, adaps, adak)
            tt("dve", ada[:, 0:2 * KC], adaps[:, 0:2 * KC], ppc("bada")[:, 0:2 * KC], ALU.add, [adak, ("pp",)], [("ada",), ("mod",)])
            stt(m1s[:], ada[:, KC:2 * KC], 1.0, ppc("n1g"), ALU.add, ALU.mult, [("ada",), ("pp",)], [("mod",)])
            S.emit_phase()

        out_stores = []
        for seg in range(cfg.NSEG):
            tok0 = seg * LS
            with ExitStack() as ph:
                set_psum(ph, 2, 4)
                xs = [sb(ph, f"xs{i}", [128, D], F32) for i in range(2)]
                tmps = norm_tmps(ph)
                if seg == 0:
                    set_ring(ph, 7)
                    adaps2, adak2 = psf[0], ("psf", 0)
                    per = (ntile - nt0 + SS - 1) // SS
                gprev1 = None
                for s in range(SS):
                    b = s % 2
                    r0 = tok0 + s * 128
                    dma("sp", xs[b][:], x_d[r0:r0 + 128, :], [], [("xs", b)], f"xs{b}")
                    g1 = rmsnorm_gen(tmps, b, xs[b][:], ("xs", b), hT, s * 128, m1s, ada[:, 0:KC], ("hT", s))
                    next(g1)
                    interleave(None, gprev1)
                    gprev1 = g1
                    if seg == 0:
                        ada_tiles(min(ntile, nt0 + s * per), min(ntile, nt0 + (s + 1) * per), adaps2, adak2)
                interleave(None, gprev1)
                if seg == 0:
                    ada_tiles(min(ntile, nt0 + SS * per), ntile, adaps2, adak2)
                    tt("dve", ada[:, 2 * KC:6 * KC], adaps2[:, 2 * KC:6 * KC], ppc("bada")[:, 2 * KC:6 * KC], ALU.add,
                       [adak2, ("pp",)], [("ada2",)])
                    stt(m2s[:], ada[:, 4 * KC:5 * KC], 1.0, ppc("n2g"), ALU.add, ALU.mult, [("ada2",), ("pp",)], [("mod2",)])
                    dgb = sb(ph, "dgb", [128, KC, 128], F32)
                    grow = sb(ph, "grow", [128, D], F32)
                    for gi in (2, 5):
                        tt("dve", dgb[:], identf[:].unsqueeze(1).to_broadcast([128, KC, 128]),
                           ada[:, gi * KC:(gi + 1) * KC].unsqueeze(2).to_broadcast([128, KC, 128]), ALU.mult,
                           [("identf",), ("ada2",)], [("dgb",)])
                        dg2 = dgb[:].rearrange("p k n -> p (k n)")
                        for q_ in range(D // 512):
                            pb_, pk_ = psf[1], ("psf", 1)
                            mmgroup(pb_[:, :], [(onesf[:], dg2[:, q_ * 512:(q_ + 1) * 512])], [("onesf",), ("dgb",)], [pk_])
                            cp("act", grow[:, q_ * 512:(q_ + 1) * 512], pb_[:, :], [pk_], [("grow",)])
                        dma("sp", grow_d[0 if gi == 2 else 1, :, :], grow[:], [("grow",)], [("growd", gi)], "grow")
                if cfg.debug:
                    dma("sp", dbg["hT"][:, :, tok0:tok0 + LS], hT[:], [("hT", s) for s in range(SS)], [], "dbg")
                S.emit_phase()

            with ExitStack() as ph:
              mgT = sb(ph, "mgT", [128, KC, LS], BF16)
              with ExitStack() as pm:
                ohgT = sb(pm, "ohgT", [128, HH, LS], BF16)
                oatT = sb(pm, "oatT", [128, AQC, LS], BF16)
                with ExitStack() as p2:
                    set_ring(p2, 2)
                    set_psum(p2, 7, 1)

                    def f32t(name):
                        return sb(p2, name, [128, TT], F32)
                    on_ = f32t("on_")
                    scc = on_
                    rmask = sb(p2, "rmask", [128, TT], BF16)
                    sqo_ = sb(p2, "sqo_", [128, TT], BF16)
                    AB = []
                    for i in range(3):
                        AB.append(dict(gs=f32t(f"gs{i}"), qt=sb(p2, f"qt{i}", [128, TT], BF16), kt=sb(p2, f"kt{i}", [128, TT], BF16),
                                       qh=sb(p2, f"qh{i}", [128, TT], BF16), kh=sb(p2, f"kh{i}", [128, TT], BF16),
                                       vsb=sb(p2, f"vsb{i}", [128, NPAIR, 128], BF16), dec=sb(p2, f"dec{i}", [128, NCH], F32),
                                       eref=sb(p2, f"eref{i}", [128, NCH], F32), elr=sb(p2, f"elr{i}", [128, NCH], F32)))
                    AO = []
                    for i in range(2):
                        AO.append({n_: f32t(f"{n_}{i}") for n_ in ("u_", "t1_", "lf_", "kv_", "b_", "e1_", "e2_", "q_")})
                        AO[-1]["bm_"] = AO[-1]["t1_"]
                    khtok = [sb(p2, f"khtok{i}", [128, NPAIR, 128], BF16) for i in range(2)]
                    scm = sb(p2, "scm", [128, NPAIR, 128], BF16)
                    Sbf = sb(p2, "Sbf", [128, NCH, 128], BF16)
                    hgo = ppc("hgo")
                    S.add("pool", lambda e: e.memset(rmask[:], 1.0), writes=[("rmask",)])
                    S.add("pool", lambda e: e.memset(rmask[:].rearrange("p (c j) -> p c j", j=64)[:, :, 0:1], 0.0), reads=[("rmask",)], writes=[("rmask",)])
                    for i in range(2):
                        S.add("pool", lambda e, i=i: e.memset(khtok[i][:], 0.0), writes=[("khtok",)])
                    hslot = {}

                    def load_head(h):
                        slot = ring_i[0] % len(ring)
                        ring_i[0] += 1
                        for j, c0 in enumerate((HQ0, HF0, HI0, HG0)):
                            wload(win_d, 0, KC, c0 + h * 128, 128, coloff=j * 128, slot=slot, kstride=512, chain=(j == 0))
                        hslot[h] = slot
                        if seg == 0:
                            issue_casts(1)

                    def stageA(h, t, pb, pa):
                        X = AB[pb]
                        Y = AO[pa]
                        slot = hslot[h]
                        wv = rview(slot, 512)
                        c0t = t * TT
                        hk = [("hT", t * NSUB + s) for s in range(NSUB)]
                        rk = [("ring", slot)]
                        fps, fk = psf[0], ("psf", 0)
                        mmgroup(fps[:, 0:TT], [(wv[:, kc, 128:256], hT[:, kc, c0t:c0t + TT]) for kc in range(KC)], hk + rk, [fk])
                        yield
                        qps, qk = psf[1], ("psf", 1)
                        mmgroup(qps[:, 0:TT], [(wv[:, kc, 0:128], hT[:, kc, c0t:c0t + TT]) for kc in range(KC)], hk + rk, [qk])
                        yield
                        gps_, gk2_ = psf[2], ("psf", 2)
                        mmgroup(gps_[:, 0:TT], [(wv[:, kc, 384:512], hT[:, kc, c0t:c0t + TT]) for kc in range(KC)], hk + rk, [gk2_])
                        yield
                        vps, vk = psf[3], ("psf", 3)
                        for s in range(NSUB):
                            mmgroup(vps[:, s * 128:(s + 1) * 128],
                                    [(hT[:, kc, c0t + s * 128:c0t + (s + 1) * 128], wv[:, kc, 256:384]) for kc in range(KC)],
                                    hk + rk, [vk])
                        yield
                        u_, t1_, lf_, kv_, b_, bm_, e1_, e2_, q_ = [Y[n_] for n_ in ("u_", "t1_", "lf_", "kv_", "b_", "bm_", "e1_", "e2_", "q_")]
                        K_ = lambda n_: (n_, pa)
                        gs_ = X["gs"]
                        act(u_[:], fps[:, 0:TT], AF.Exp, [fk], [K_("u_")], scale=-1.0)
                        act(e1_[:], qps[:, 0:TT], AF.Exp, [qk], [K_("e1_")], scale=-1.0)
                        act(gs_[:], gps_[:, 0:TT], AF.Exp, [gk2_], [("gs_", pb)], scale=-1.0)
                        cp("act", X["vsb"][:], vps[:, 0:TT].rearrange("p (a v) -> p a v", v=128), [vk], [("vsb", pb)])
                        act(e1_[:], e1_[:], AF.Ln, [K_("e1_")], [K_("e1_")], bias=1.0)
                        act(e1_[:], e1_[:], AF.Exp, [K_("e1_")], [K_("e1_")], scale=-1.0)
                        tt("dve", q_[:], qps[:, 0:TT], e1_[:], ALU.mult, [qk, K_("e1_")], [K_("q_")])
                        act(gs_[:], gs_[:], AF.Ln, [("gs_", pb)], [("gs_", pb)], bias=1.0)
                        act(gs_[:], gs_[:], AF.Exp, [("gs_", pb)], [("gs_", pb)], scale=-1.0)
                        tt("dve", gs_[:], gps_[:, 0:TT], gs_[:], ALU.mult, [gk2_, ("gs_", pb)], [("gs_", pb)])
                        act(t1_[:], u_[:], AF.Ln, [K_("u_"), ("lb",)], [K_("t1_")], bias=1.0, scale=lb[:, h:h + 1])
                        act(lf_[:], u_[:], AF.Ln, [K_("u_")], [K_("lf_")], bias=1.0)
                        tt("dve", lf_[:], t1_[:], lf_[:], ALU.subtract, [K_("t1_"), K_("lf_")], [K_("lf_")])
                        S.add("dve", lambda e: e.tensor_tensor_scan(out=b_[:], data0=rmask[:], data1=lf_[:], initial=0.0,
                                                                   op0=ALU.mult, op1=ALU.add), reads=[K_("lf_"), ("rmask",)], writes=[K_("b_")])
                        act(kv_[:], lf_[:], AF.Exp, [K_("lf_")], [K_("kv_")])
                        ts("dve", kv_[:], kv_[:], -1.0, 1.0, ALU.mult, ALU.add, [K_("kv_")], [K_("kv_")])
                        b3 = b_[:].rearrange("p (c j) -> p c j", j=64)
                        bm3 = bm_[:].rearrange("p (c j) -> p c j", j=64)
                        tt("dve", bm3, b3, b3[:, :, 31:32].to_broadcast([128, NCH, 64]), ALU.subtract, [K_("b_")], [K_("t1_")])
                        ts("dve", bm_[:], bm_[:], 80.0, -80.0, ALU.min, ALU.max, [K_("t1_")], [K_("t1_")])
                        act(u_[:], bm_[:], AF.Exp, [K_("t1_"), K_("u_")], [K_("u_")])
                        act(e2_[:], bm_[:], AF.Exp, [K_("t1_")], [K_("e2_")], scale=-1.0)
                        act(X["eref"][:], b3[:, :, 31], AF.Exp, [K_("b_")], [("eref", pb)])
                        tt("dve", X["elr"][:], b3[:, :, 63], b3[:, :, 31], ALU.subtract, [K_("b_")], [("elr", pb)])
                        act(X["elr"][:], X["elr"][:], AF.Exp, [("elr", pb)], [("elr", pb)])
                        act(X["dec"][:], b3[:, :, 63], AF.Exp, [K_("b_")], [("dec", pb)])
                        tt("pool", X["qt"][:], q_[:], u_[:], ALU.mult, [K_("q_"), K_("u_")], [("qt_", pb)])
                        tt("dve", X["kt"][:], kv_[:], e2_[:], ALU.mult, [K_("kv_"), K_("e2_")], [("kt_", pb)])
                        qt3 = X["qt"][:].rearrange("p (c j) -> p c j", j=64)
                        kt3 = X["kt"][:].rearrange("p (c j) -> p c j", j=64)
                        tt("pool", X["qh"][:].rearrange("p (c j) -> p c j", j=64), qt3, X["eref"][:].unsqueeze(2).to_broadcast([128, NCH, 64]),
                           ALU.mult, [("qt_", pb), ("eref", pb)], [("qh_", pb)])
                        tt("dve", X["kh"][:].rearrange("p (c j) -> p c j", j=64), kt3, X["elr"][:].unsqueeze(2).to_broadcast([128, NCH, 64]),
                           ALU.mult, [("kt_", pb), ("elr", pb)], [("kh_", pb)])

                    BB = [4, 5, 6]

                    def stageB(h, t, pb):
                        X = AB[pb]
                        c0t = t * TT
                        qt_, kt_, qh_, kh_, vsb, dec, gs_ = X["qt"], X["kt"], X["qh"], X["kh"], X["vsb"], X["dec"], X["gs"]
                        pb_, pk = bankb()

                        def trk(e, pb_=pb_):
                            ins = None
                            for a in range(NPAIR):
                                ins = e.transpose(out=pb_[:, a * 128:(a + 1) * 128], in_=kh_[:, a * 128:(a + 1) * 128], identity=identb[:])
                            return ins
                        S.add("pe", trk, reads=[("kh_", pb), ("identb",)], writes=[pk])
                        cp("dve", khtok[0][0:64, :, :], pb_[0:64, 0:TT].rearrange("p (a v) -> p a v", v=128), [pk], [("khtok",)])
                        cp("dve", khtok[1][64:128, :, :], pb_[64:128, 0:TT].rearrange("p (a v) -> p a v", v=128), [pk], [("khtok",)])
                        scps, sck = bankp("B", BB)
                        for a in range(NPAIR):
                            mmgroup(scps[:, a * 128:(a + 1) * 128], [(kt_[:, a * 128:(a + 1) * 128], qt_[:, a * 128:(a + 1) * 128])],
                                    [("kt_", pb), ("qt_", pb)], [sck])
                        ts("dve", scc[:], scps[:, 0:TT], 1e30, -1e30, ALU.min, ALU.max, [sck], [("on_",)])
                        tt("dve", scm[:], scc[:].rearrange("p (a v) -> p a v", v=128),
                           hgmask[:].unsqueeze(1).to_broadcast([128, NPAIR, 128]), ALU.mult, [("on_",), ("hgmask",)], [("scm",)])
                        yield
                        for half in range((NCH + 3) // 4):
                            ups, uk = bankp("B", BB)
                            nch = min(4, NCH - 4 * half)
                            for j in range(nch):
                                ch = 4 * half + j
                                a = ch // 2
                                mmgroup(ups[:, j * 128:(j + 1) * 128], [(khtok[ch % 2][:, a, :], vsb[:, a, :])],
                                        [("khtok",), ("vsb", pb)], [uk])
                            for j in range(nch):
                                ch = 4 * half + j
                                gch = t * NCH + ch + seg * (LS // 64)
                                cur, nxt = gch % 2, (gch + 1) % 2
                                cp("pool", Sbf[:, ch, :], Sst[cur][:, h, :], [("Sst", cur, h)], [("Sbf", ch)])
                                stt(Sst[nxt][:, h, :], Sst[cur][:, h, :], dec[:, ch:ch + 1], ups[:, j * 128:(j + 1) * 128],
                                    ALU.mult, ALU.add, [("Sst", cur, h), ("dec", pb), uk], [("Sst", nxt, h)])
                        yield
                        ops_, ok_ = bankp("B", BB)
                        for a in range(NPAIR):
                            mmgroup(ops_[:, a * 128:(a + 1) * 128], [(vsb[:, a, :], scm[:, a, :])], [("vsb", pb), ("scm",)], [ok_])

                            def inter(e, a=a, ops_=ops_):
                                ins = None
                                for r in range(2):
                                    ch = 2 * a + r
                                    ins = e.matmul(ops_[:, ch * 64:(ch + 1) * 64], lhsT=Sbf[:, ch, :], rhs=qh_[:, ch * 64:(ch + 1) * 64],
                                                   start=False, stop=True, skip_group_check=True)
                                return ins
                            S.add("pe", inter, reads=[("Sbf", 2 * a), ("Sbf", 2 * a + 1), ("qh_", pb)], writes=[ok_])
                        act(sqo_[:], ops_[:, 0:TT], AF.Square, [ok_], [("sqo_",)])
                        yield
                        sps, sk = bankp("B", BB)
                        mmgroup(sps[:, 0:TT], [(onesb[:], sqo_[:])], [("onesb",), ("sqo_",)], [sk])
                        act(on_[:], sps[:, 0:TT], AF.Ln, [sk], [("on_",)], bias=EPS, scale=1.0 / 128)
                        act(on_[:], on_[:], AF.Exp, [("on_",)], [("on_",)], scale=-0.5)
                        tt("dve", on_[:], ops_[:, 0:TT], on_[:], ALU.mult, [ok_, ("on_",)], [("on_",)])
                        stt(ohgT[:, h, c0t:c0t + TT], on_[:], hgo[:, 0:1], gs_[:], ALU.mult, ALU.mult,
                            [("on_",), ("gs_", pb), ("pp",)], [("ohg", h, t)])

                    its = [(h, t) for h in range(HH) for t in range(NT)]
                    LAG = 2

                    def adv(g):
                        if g is not None:
                            try:
                                next(g)
                            except StopIteration:
                                pass

                    load_head(0)
                    gprev = None
                    for i, (h, t) in enumerate(its):
                        if t == 0 and h + 1 < HH:
                            load_head(h + 1)
                        j = i - LAG
                        gnew = stageB(its[j][0], its[j][1], j % 3) if j >= 0 else None
                        ga = stageA(h, t, i % 3, i % 2)
                        adv(ga); adv(gprev)
                        adv(ga); adv(gprev); adv(gprev)
                        adv(ga); adv(gnew)
                        adv(ga); adv(gnew)
                        interleave(ga, None)
                        gprev = gnew
                    interleave(None, gprev)
                    for j in range(max(0, len(its) - LAG), len(its)):
                        interleave(None, stageB(its[j][0], its[j][1], j % 3))
                    S.emit_phase()
                with ExitStack() as p3:
                    set_ring(p3, 2)
                    set_psum(p3, 8, 0)
                    NB = LS // 128
                    KH = []
                    for i in range(2):
                        d_ = dict(qTg=[sb(p3, f"qTg{i}_{j}", [128, 2, LS], BF16) for j in range(2)],
                                  kdT=sb(p3, f"kdT{i}", [128, 128 + LS], BF16), vtok=sb(p3, f"vtok{i}", [128, NB + 1, 64], BF16))
                        for j in range(2):
                            S.add("pool", lambda e, q_=d_["qTg"][j]: e.memset(q_[:], 0.0),
                                  writes=[("qTg", i, c_, t_) for c_ in range(2) for t_ in range(NT)])
                        KH.append(d_)
                    for i in range(2):
                        KH[i]["bH"] = sb(p3, f"bH{i}", [128, 2, 4, 128], BF16)
                        KH[i]["bL"] = sb(p3, f"bL{i}", [128, 2, 4, 128], BF16)
                    bstage = sb(p3, "bstage", [128, 2, 4, 128], F32)
                    TA = [dict(sq=sb(p3, f"sq{i}", [128, TT], BF16), r=sb(p3, f"r{i}", [128, TT], F32)) for i in range(2)]
                    TB = [dict(pT=sb(p3, f"pT{i}", [128, 512], BF16)) for i in range(2)]
                    DN = [sb(p3, f"dn{i}", [128, TT], F32) for i in range(2)]
                    gk_c = ppc("gk")
                    kslot = {}
                    cnt3 = {"a": 0, "b": 0, "d": 0}

                    def load_w(kh):
                        slot = ring_i[0] % len(ring)
                        ring_i[0] += 1
                        wload(win_d, 0, KC, AQ0 + kh * 256, 256, coloff=0, slot=slot, kstride=512)
                        wload(win_d, 0, KC, AK0 + kh * 64, 64, coloff=256, slot=slot, kstride=512, chain=False)
                        wload(win_d, 0, KC, AV0 + kh * 64, 64, coloff=320, slot=slot, kstride=512, chain=False)
                        kslot[kh] = slot
                        if seg == 0:
                            issue_casts(2)

                    def load_aux(kh):
                        kp = kh % 2
                        for hl in range(4):
                            hd = 4 * kh + hl
                            for cpv in range(2):
                                src = bass.AP(tensor=z_d.tensor, offset=z_d[hd, cpv, 1, 0].offset, ap=[[255, 128], [1, 128]])
                                dma("sp", bstage[:, hl // 2, (hl % 2) * 2 + (1 - cpv), :], src, [("z", cpv)], [("bstage",)], "bias")
                        bs2 = bstage[:].rearrange("p a b q -> p (a b q)")
                        bh2 = KH[kp]["bH"][:].rearrange("p a b q -> p (a b q)")
                        bl2 = KH[kp]["bL"][:].rearrange("p a b q -> p (a b q)")
                        cp("dve", bh2, bs2, [("bstage",)], [("bH", kp)])
                        tt("dve", bl2, bs2, bh2, ALU.subtract, [("bstage",), ("bH", kp)], [("bL", kp)])
                        if seg > 0:
                            cp("pool", KH[kp]["kdT"][:, 0:128], kcar[:, kh, :], [("kcar", kh)], [("kdT", kp, -1)])
                            cp("pool", KH[kp]["vtok"][:, 0, :], vcar[:, kh, :], [("vcar", kh)], [("vtok", kp, -1)])

                    def stageA3(kh, t):
                        kp = kh % 2
                        Kq, kdT, vtok = KH[kp]["qTg"], KH[kp]["kdT"], KH[kp]["vtok"]
                        wv = rview(kslot[kh], 512)
                        rk = [("ring", kslot[kh])]
                        c0t = t * TT
                        hk = [("hT", t * NSUB + s) for s in range(NSUB)]
                        for c in range(3):
                            ta = cnt3["a"] % 2
                            cnt3["a"] += 1
                            sq_, r_ = TA[ta]["sq"], TA[ta]["r"]
                            pps, ppk = psf[ta], ("psf", ta)
                            if c < 2:
                                mmgroup(pps[:, 0:TT], [(wv[:, kc, c * 128:(c + 1) * 128], hT[:, kc, c0t:c0t + TT]) for kc in range(KC)],
                                        hk + rk, [ppk])
                            else:
                                for r in (0, 64):
                                    mmgroup(pps[r:r + 64, 0:TT], [(wv[:, kc, 256:320], hT[:, kc, c0t:c0t + TT]) for kc in range(KC)],
                                            hk + rk, [ppk])
                            act(sq_[:], pps[:, 0:TT], AF.Square, [ppk], [("sq_", ta)])
                            sps, sk = psf[2], ("psf", 2)
                            mmgroup(sps[:, 0:TT], [(bdb[:], sq_[:])], [("bdb",), ("sq_", ta)], [sk])
                            yield
                            act(r_[:], sps[:, 0:TT], AF.Ln, [sk], [("r_", ta)], bias=EPS, scale=1.0 / 64)
                            act(r_[:], r_[:], AF.Exp, [("r_", ta)], [("r_", ta)], scale=-0.5)
                            if c < 2:
                                for r2 in range(2):
                                    rr = slice(r2 * 64, r2 * 64 + 64)
                                    stt(Kq[r2][rr, c, c0t:c0t + TT], pps[rr, 0:TT], gq[rr, 0:1], r_[rr, :], ALU.mult, ALU.mult,
                                        [ppk, ("r_", ta), ("gq",)], [("qTg", kp, c, t)])
                            else:
                                stt(kdT[:, 128 + c0t:128 + c0t + TT], pps[:, 0:TT], gk_c[:, 0:1], r_[:], ALU.mult, ALU.mult,
                                    [ppk, ("r_", ta), ("pp",)], [("kdT", kp, t)])
                        vps, vk = psf[3], ("psf", 3)
                        for s in range(NSUB):
                            mmgroup(vps[:, s * 64:(s + 1) * 64],
                                    [(hT[:, kc, c0t + s * 128:c0t + (s + 1) * 128], wv[:, kc, 320:384]) for kc in range(KC)], hk + rk, [vk])
                        cp("act", vtok[:, 1 + t * NSUB:1 + (t + 1) * NSUB, :], vps[:, 0:NSUB * 64].rearrange("p (a v) -> p a v", v=64),
                           [vk], [("vtok", kp, t)])
                        yield

                    def stageB3(kh, t):
                        kp = kh % 2
                        Kq, kdT, vtok = KH[kp]["qTg"], KH[kp]["kdT"], KH[kp]["vtok"]
                        c0t = t * TT
                        for pr in range(2):
                            hp = 2 * kh + pr
                            nps, nk_ = psf[4], ("psf", 4)
                            dps, dk_ = psf[5], ("psf", 5)
                            bH4 = KH[kp]["bH"][:, pr, :, :].rearrange("p a q -> p (a q)")
                            bL4 = KH[kp]["bL"][:, pr, :, :].rearrange("p a q -> p (a q)")

                            def logits(bl):
                                n = t * NSUB + bl
                                first = (seg == 0 and n == 0)
                                tb = cnt3["b"] % 2
                                cnt3["b"] += 1
                                pT_ = TB[tb]["pT"]
                                lps, lk = psf[6 + tb], ("psf", 6 + tb)
                                pt_ = t - 1 if (bl == 0 and t > 0) else (-1 if bl == 0 else t)
                                kreads = [("kdT", kp, t), ("kdT", kp, pt_), ("qTg", kp, pr, t), ("bH", kp), ("bL", kp), ("identb",)]
                                for r2 in range(2):
                                    for pc in range(2):
                                        if pc == 0 and first:
                                            continue
                                        kc0 = n * 128 + pc * 128
                                        sl_ = slice((r2 * 2 + pc) * 128, (r2 * 2 + pc + 1) * 128)
                                        mmgroup(lps[:, sl_], [(kdT[:, kc0:kc0 + 128], Kq[r2][:, pr, n * 128:(n + 1) * 128]),
                                                              (identb[:], bH4[:, sl_]), (identb[:], bL4[:, sl_])], kreads, [lk])
                                sl = [slice(0, 512)] if not first else [slice(128, 256), slice(384, 512)]
                                for s_ in sl:
                                    act(pT_[:, s_], lps[:, s_], AF.Exp, [lk], [("pT_", tb)])
                                return (bl, n, first, tb, pt_)

                            def pv(bl, n, first, tb, pt_):
                                pT_ = TB[tb]["pT"]
                                vreads = [("vtok", kp, t), ("vtok", kp, pt_), ("pT_", tb)]
                                for r2 in range(2):
                                    r = r2 * 64
                                    prs = []
                                    drs = []
                                    for pc in range(2):
                                        if pc == 0 and first:
                                            continue
                                        pslice = pT_[:, (r2 * 2 + pc) * 128:(r2 * 2 + pc + 1) * 128]
                                        prs.append((vtok[:, n + pc, :], pslice))
                                        drs.append((onesb[:, 0:64], pslice))
                                    mmgroup(nps[r:r + 64, bl * 128:(bl + 1) * 128], prs, vreads, [nk_])
                                    mmgroup(dps[r:r + 64, bl * 128:(bl + 1) * 128], drs, vreads + [("onesb",)], [dk_])

                            prev = None
                            for bl in range(NSUB):
                                cur = logits(bl)
                                if prev is not None:
                                    pv(*prev)
                                prev = cur
                                yield
                            pv(*prev)
                            di = cnt3["d"] % 2
                            cnt3["d"] += 1
                            dn_ = DN[di]
                            act(dn_[:], dps[:, 0:TT], AF.Ln, [dk_, ("esel",)], [("dn_", di)], bias=esel[:, hp:hp + 1])
                            act(dn_[:], dn_[:], AF.Exp, [("dn_", di)], [("dn_", di)], scale=-1.0)
                            tt("dve", oatT[:, hp, c0t:c0t + TT], nps[:, 0:TT], dn_[:], ALU.mult, [nk_, ("dn_", di)], [("oat", hp, t)])
                            yield
                        if t == NT - 1 and seg + 1 < cfg.NSEG:
                            cp("pool", kcar[:, kh, :], kdT[:, LS:LS + 128], [("kdT", kp, NT - 1)], [("kcar", kh)])
                            cp("pool", vcar[:, kh, :], vtok[:, NB, :], [("vtok", kp, NT - 1)], [("vcar", kh)])

                    its3 = [(kh, t) for kh in range(KVH) for t in range(NT)]
                    load_w(0)
                    load_aux(0)
                    for i, (kh, t) in enumerate(its3):
                        if t == 0 and kh + 1 < KVH:
                            load_w(kh + 1)
                        gb = stageB3(*its3[i - 1]) if i > 0 else None
                        interleave(stageA3(kh, t), gb, nb_per_a=3)
                        if t == 0 and kh + 1 < KVH:
                            load_aux(kh + 1)
                    interleave(None, stageB3(*its3[-1]))
                    S.emit_phase()
                if cfg.debug:
                    dma("sp", dbg["ohg"][:, :, tok0:tok0 + LS], ohgT[:], [("ohg", h, t) for h in range(HH) for t in range(NT)], [], "dbg")
                    dma("sp", dbg["oat"][:, :, tok0:tok0 + LS], oatT[:], [("oat", c, t) for c in range(AQC) for t in range(NT)], [], "dbg")
                with ExitStack() as p4:
                    set_ring(p4, 8, width=256)
                    set_psum(p4, 8, 0)
                    P4T = [dict(s1=sb(p4, f"s1_{i}", [128, TT], F32), s2=sb(p4, f"s2_{i}", [128, TT], F32),
                                m1=sb(p4, f"m1_{i}", [128, TT], F32)) for i in range(2)]
                    it4 = 0
                    CW = 256
                    for ct in range(D // CW):
                        sl_gh = wload(win_d, 0, KC, GH0 + ct * CW, CW)
                        sl_ga = wload(win_d, 0, KC, GA0 + ct * CW, CW)
                        sl_br = ring_i[0] % len(ring)
                        ring_i[0] += 1
                        wload(wbh_d, 0, HH, ct * CW, CW, slot=sl_br, kcoff=0)
                        wload(wba_d, 0, AQC, ct * CW, CW, slot=sl_br, kcoff=HH, chain=False)
                        if seg == 0 and ct % 2 == 1:
                            issue_casts(3)
                        wgh, wga, wbr = rview(sl_gh, CW), rview(sl_ga, CW), rview(sl_br, CW)
                        for j in range(CW // 128):
                            n_ = ct * (CW // 128) + j
                            for t in range(NT):
                                c0t = t * TT
                                pb4 = it4 % 2
                                it4 += 1
                                s1_, s2_, m1_ = P4T[pb4]["s1"], P4T[pb4]["s2"], P4T[pb4]["m1"]
                                hk = [("hT", t * NSUB + s) for s in range(NSUB)]
                                ghps, ghk = bank()
                                mmgroup(ghps[:, 0:TT], [(wgh[:, kc, j * 128:(j + 1) * 128], hT[:, kc, c0t:c0t + TT]) for kc in range(KC)],
                                        hk + [("ring", sl_gh)], [ghk])
                                gaps, gak = bank()
                                mmgroup(gaps[:, 0:TT], [(wga[:, kc, j * 128:(j + 1) * 128], hT[:, kc, c0t:c0t + TT]) for kc in range(KC)],
                                        hk + [("ring", sl_ga)], [gak])
                                act(s1_[:], ghps[:, 0:TT], AF.Exp, [ghk], [("s1_", pb4)], scale=-1.0)
                                act(s2_[:], gaps[:, 0:TT], AF.Exp, [gak], [("s2_", pb4)], scale=-1.0)
                                bhps, bhk = bank()
                                mmgroup(bhps[:, 0:TT], [(wbr[:, k, j * 128:(j + 1) * 128], ohgT[:, k, c0t:c0t + TT]) for k in range(HH)],
                                        [("ohg", k, t) for k in range(HH)] + [("ring", sl_br)], [bhk])
                                baps, bak = bank()
                                mmgroup(baps[:, 0:TT], [(wbr[:, HH + k, j * 128:(j + 1) * 128], oatT[:, k, c0t:c0t + TT]) for k in range(AQC)],
                                        [("oat", k, t) for k in range(AQC)] + [("ring", sl_br)], [bak])
                                for (sx, sxk) in ((s1_, ("s1_", pb4)), (s2_, ("s2_", pb4))):
                                    act(sx[:], sx[:], AF.Ln, [sxk], [sxk], bias=1.0)
                                    act(sx[:], sx[:], AF.Exp, [sxk], [sxk], scale=-1.0)
                                tt("dve", m1_[:], bhps[:, 0:TT], s1_[:], ALU.mult, [bhk, ("s1_", pb4)], [("m1_", pb4)])
                                tt("dve", s2_[:], baps[:, 0:TT], s2_[:], ALU.mult, [bak, ("s2_", pb4)], [("s2_", pb4)])
                                tt("dve", mgT[:, n_, c0t:c0t + TT], m1_[:], s2_[:], ALU.add, [("m1_", pb4), ("s2_", pb4)], [("mg", t)])
                    if cfg.debug:
                        dma("sp", dbg["mg"][:, :, tok0:tok0 + LS], mgT[:], [("mg", t) for t in range(NT)], [], "dbg")
                    S.emit_phase()
              if True:
                with ExitStack() as p5:
                    set_ring(p5, max(4, D // 512))
                    set_psum(p5, 4, 4)
                    xs = [sb(p5, f"xs5_{i}", [128, D], F32) for i in range(2)]
                    x1s = [sb(p5, f"x1s_{i}", [128, D], F32) for i in range(2)]
                    g1row = sb(p5, "g1row", [128, D], F32)
                    dma("sp", g1row[:], grow_d[0, :, :], [("growd", 2)], [("g1row",)], "grow")
                    tmps = norm_tmps(p5)
                    ty = [sb(p5, f"ty{i}", [128, 512], F32) for i in range(2)]
                    wslots = [wload(wout_d, 0, KC, ct * 512, 512) for ct in range(D // 512)]
                    if seg == 0:
                        issue_casts(NCAST)
                    assert len(wslots) <= len(ring)

                    def projP(s):
                        b = s % 2
                        r0 = tok0 + s * 128
                        t = s // NSUB
                        cs = s * 128
                        dma("sp", xs[b][:], x_d[r0:r0 + 128, :], [], [("xs5", b)], f"xs{b}")
                        for ct in range(D // 512):
                            wv = rview(wslots[ct], 512)
                            yps, yk = bank()
                            mmgroup(yps[:, :], [(mgT[:, kc, cs:cs + 128], wv[:, kc, :]) for kc in range(KC)],
                                    [("mg", t), ("ring", wslots[ct])], [yk])
                            tyb = ty[ct % 2]
                            tt("dve", tyb[:], yps[:, :], g1row[:, ct * 512:(ct + 1) * 512], ALU.mult, [yk, ("g1row",)], [("ty", ct % 2)])
                            tt("pool", x1s[b][:, ct * 512:(ct + 1) * 512], tyb[:], xs[b][:, ct * 512:(ct + 1) * 512], ALU.add,
                               [("ty", ct % 2), ("xs5", b)], [("x1s", b)])
                            yield
                        dma("sp", x1_d[r0:r0 + 128, :], x1s[b][:], [("x1s", b)], [("x1d", seg, s)], f"x1st{b}")
                        if cfg.debug:
                            dma("sp", dbg["x1"][r0:r0 + 128, :], x1s[b][:], [("x1s", b)], [], "dbg")

                    def normN(s):
                        b = s % 2
                        return rmsnorm_gen(tmps, b, x1s[b][:], ("x1s", b), hT, s * 128, m2s, ada[:, 3 * KC:4 * KC], ("hT", s), modkey=("mod2",))

                    for s in range(SS):
                        interleave(projP(s), normN(s - 1) if s > 0 else None)
                    interleave(None, normN(SS - 1))
                    S.emit_phase()

            with ExitStack() as p6:
                set_ring(p6, 4)
                set_psum(p6, 8, 0)
                aT = sb(p6, "aT", [128, FC, TT], BF16)
                rl = [sb(p6, f"rl{i}", [128, TT], F32) for i in range(2)]
                xp = [sb(p6, f"xp{i}", [128, 512], F32) for i in range(2)]
                to = sb(p6, "to", [128, 512], F32)
                osb = [sb(p6, f"osb{i}", [128, 512], F32) for i in range(2)]
                g2row = sb(p6, "g2row", [128, D], F32)
                dma("sp", g2row[:], grow_d[1, :, :], [("growd", 5)], [("g2row",)], "grow")
                cnt = 0
                for t in range(NT):
                    c0t = t * TT
                    hk = [("hT", t * NSUB + s) for s in range(NSUB)]
                    for fct in range(DFF // 512):
                        slot = wload_c(wff1c_d, 0, KC, fct * 512, 512, ("wc1", fct))
                        wv = rview(slot, 512)
                        for j in range(4):
                            fc = fct * 4 + j
                            aps, ak_ = bank()
                            mmgroup(aps[:, 0:TT], [(wv[:, kc, j * 128:(j + 1) * 128], hT[:, kc, c0t:c0t + TT]) for kc in range(KC)],
                                    hk + [("ring", slot)], [ak_])
                            rb = fc % 2
                            act(rl[rb][:], aps[:, 0:TT], AF.Relu, [ak_], [("rl", rb)])
                            tt("dve", aT[:, fc, :], rl[rb][:], rl[rb][:], ALU.mult, [("rl", rb)], [("aT", fc // KC)])
                    NKB = FC // KC
                    for pn in range(D // 512):
                        accs = [bank() for _ in range(NSUB)]
                        for kb in range(NKB):
                            slot = wload_c(wff2c_d, kb * KC * 128, KC, pn * 512, 512, ("wc2", kb, pn))
                            wv = rview(slot, 512)
                            for s in range(NSUB):
                                def f2(e, s=s, kb=kb, wv=wv, acc=accs[s][0]):
                                    ins = None
                                    for kc in range(KC):
                                        ins = e.matmul(acc[:, :], lhsT=aT[:, kb * KC + kc, s * 128:(s + 1) * 128], rhs=wv[:, kc, :],
                                                       start=(kb == 0 and kc == 0), stop=(kb == NKB - 1 and kc == KC - 1))
                                    return ins
                                S.add("pe", f2, reads=[("aT", kb), ("ring", slot)], writes=[accs[s][1]])
                        for s in range(NSUB):
                            sg = t * NSUB + s
                            r0 = tok0 + sg * 128
                            b = cnt % 2
                            cnt += 1
                            dma("sp", xp[b][:], x1_d[r0:r0 + 128, pn * 512:(pn + 1) * 512], [("x1d", seg, sg)], [("xp", b)], f"xp{b}")
                            tt("dve", to[:], accs[s][0][:, :], g2row[:, pn * 512:(pn + 1) * 512], ALU.mult, [accs[s][1], ("g2row",)], [("to",)])
                            tt("dve", osb[b][:], to[:], xp[b][:], ALU.add, [("to",), ("xp", b)], [("osb", b)])
                            out_stores.append(dma("sp", out_d[r0:r0 + 128, pn * 512:(pn + 1) * 512], osb[b][:], [("osb", b)], [], f"ost{b}"))
                last = (seg == cfg.NSEG - 1)
                fw = []
                if last:
                    fw = [out_stores[-1], out_stores[-2]]
                    if cfg.debug:
                        fw.append(S.add("sp", lambda e: e.dma_start(out=dbg["ada"][:, :], in_=ada[:]), reads=[], writes=[], dma_slot="dbg"))
                S.emit_phase(final_waits=fw)
    except StopBuild:
        pass
    return nc


def host_pack(cfg, b, c, b_ada, norm1_g, norm2_g, hg_lb_logits, hg_out_norm_g, q_norm_g, k_norm_g, attn_sinks):
    KC, HH, AH = cfg.KC, cfg.HH, cfg.AH
    ohg, ident, hgmask, bd = host_consts(cfg)
    pp = np.zeros((128, cfg.NP), np.float32)

    def put(name, arr):
        a, bb = cfg.ppoff[name]
        pp[:, a:bb] = arr
    put("c", c[b].reshape(KC, 128).T)
    put("bada", b_ada.reshape(6 * KC, 128).T)
    put("n1g", norm1_g.reshape(KC, 128).T)
    put("n2g", norm2_g.reshape(KC, 128).T)
    put("lb0", hg_lb_logits[0].reshape(HH, 128).T)
    put("lb1", hg_lb_logits[1].reshape(HH, 128).T)
    put("hgo", hg_out_norm_g.reshape(128, 1))
    put("gq", np.tile(q_norm_g.reshape(64), 2).reshape(128, 1))
    put("gk", np.tile(k_norm_g.reshape(64), 2).reshape(128, 1))
    sk = attn_sinks.reshape(AH // 2, 2)
    put("sink", np.concatenate([np.repeat(sk[:, 0][None, :], 64, 0), np.repeat(sk[:, 1][None, :], 64, 0)], 0))
    put("ident", ident)
    put("hgmask", hgmask)
    put("bd", bd)
    return pp, ohg


def run(cfg, inputs, runner=None, n_cores=None):
    f = lambda k: np.asarray(inputs[k], dtype=np.float32)
    x, c = f("x"), f("c")
    B = x.shape[0]
    nc = build(cfg)
    tabaug = np.concatenate([f("rel_bias_table"), np.ones((1, cfg.AH), np.float32)], 0)
    shared = {
        "w_ada": np.ascontiguousarray(f("w_ada")[0]), "w_in": np.ascontiguousarray(f("w_in")[0]),
        "w_bh": np.ascontiguousarray(f("w_branch_hg")[0]), "w_ba": np.ascontiguousarray(f("w_branch_attn")[0]),
        "w_out": np.ascontiguousarray(f("w_out")[0]), "w_ff1": np.ascontiguousarray(f("w_ff1")[0]),
        "w_ff2": np.ascontiguousarray(f("w_ff2")[0]), "tabaug": tabaug,
    }
    in_maps = []
    for b in range(B):
        pp, ohg = host_pack(cfg, b, c, f("b_ada")[0], f("norm1_g")[0], f("norm2_g")[0], f("hg_lb_logits"),
                            f("hg_out_norm_g")[0], f("q_norm_g")[0], f("k_norm_g")[0], f("attn_sinks")[0])
        m = dict(shared)
        m.update({"x": np.ascontiguousarray(x[b]), "pp": pp, "ohg": ohg})
        in_maps.append(m)
    if runner is not None:
        return runner(nc, in_maps)
    res = run_bass_kernel_spmd(nc, in_maps, core_ids=list(range(B)))
    return np.stack([np.asarray(r["out"], dtype=np.float32) for r in res.results], 0)


def kernel(**inputs):
    return run(Cfg(), inputs)
```

```python
from contextlib import ExitStack
import math
import numpy as np
import concourse.bass as bass
import concourse.mybir as mybir
from concourse.bass_utils import run_bass_kernel_spmd

F32 = mybir.dt.float32
BF16 = mybir.dt.bfloat16
ALU = mybir.AluOpType
AF = mybir.ActivationFunctionType
EPS = 1e-6
NEG = -30000.0


class StopBuild(Exception):
    pass


class Op:
    __slots__ = ("eng", "fn", "deps", "signaled", "seq", "is_dma", "dma_slot", "dma_val", "phase", "slotmax")


class Sched:
    ENG = ("pe", "act", "dve", "pool", "sp")

    def __init__(self, nc, es):
        self.nc = nc
        self.last_w = {}
        self.readers = {}
        self.dma_cnt = {}
        self.last_dma = {}
        self.dsems = {}
        self.es = es
        self.sems = {e: es.enter_context(nc.semaphore(f"s_{e}")) for e in self.ENG}
        self.seqc = {e: 0 for e in self.ENG}
        self.phase = 0
        self.streams = {e: [] for e in self.ENG}
        self.nops = 0
        self.stop_after = None
        self.op_limit = None
        self.oplog = []

    def add(self, eng, fn, reads=(), writes=(), dma_slot=None, extra_deps=(), chain=True):
        if self.op_limit is not None and self.nops >= self.op_limit:
            op = Op()
            op.eng, op.is_dma, op.phase, op.signaled, op.dma_slot = eng, False, -1, False, None
            return op
        if self.op_limit is not None:
            import sys as _sys
            f = _sys._getframe(1)
            while f is not None and f.f_code.co_name != "build":
                f = f.f_back
            self.oplog.append((self.nops, eng, f.f_lineno if f else -1))
        op = Op()
        op.eng, op.fn, op.signaled, op.seq = eng, fn, False, 0
        op.is_dma = dma_slot is not None
        op.dma_slot, op.dma_val, op.phase = dma_slot, 0, self.phase
        deps, seen = [], set()

        def add_dep(d):
            if d is None or id(d) in seen:
                return
            seen.add(id(d))
            if not d.is_dma:
                if d.phase != self.phase:
                    return
                if (not op.is_dma) and d.eng == "pe" and eng == "pe":
                    return
            deps.append(d)

        for r in reads:
            add_dep(self.last_w.get(r))
        for r in writes:
            add_dep(self.last_w.get(r))
            for rd in self.readers.get(r, {}).values():
                add_dep(rd)
        for d in extra_deps:
            add_dep(d)
        if op.is_dma:
            if chain:
                add_dep(self.last_dma.get(dma_slot))
            self.last_dma[dma_slot] = op
        op.deps = deps
        op.slotmax = {d.dma_slot: 16 * self.dma_cnt[d.dma_slot] for d in deps if d.is_dma}
        for d in deps:
            if not d.is_dma:
                d.signaled = True
        for r in reads:
            key = ("dma", dma_slot) if op.is_dma else eng
            self.readers.setdefault(r, {})[key] = op
        for r in writes:
            self.last_w[r] = op
            self.readers[r] = {}
        if op.is_dma:
            if dma_slot not in self.dsems:
                self.dsems[dma_slot] = self.es.enter_context(self.nc.semaphore(f"d_{dma_slot}"))
            c = self.dma_cnt.get(dma_slot, 0) + 1
            self.dma_cnt[dma_slot] = c
            op.dma_val = 16 * c
        self.streams[eng].append(op)
        self.nops += 1
        return op

    def emit_phase(self, final_waits=()):
        nc = self.nc
        lastd = {}
        for e in self.ENG:
            for op in self.streams[e]:
                if op.is_dma:
                    lastd[op.dma_slot] = op
        fw = list(final_waits) + list(lastd.values())
        if fw:
            self.add("sp", None, extra_deps=fw)
        for e in self.ENG:
            k = self.seqc[e]
            for op in self.streams[e]:
                if op.signaled and not op.is_dma:
                    k += 1
                    op.seq = k
            self.seqc[e] = k
        streams = self.streams
        sems, dsems = self.sems, self.dsems

        def run_stream(e, eng):
            waited = {}
            for op in streams[e]:
                for d in op.deps:
                    if d.is_dma:
                        key, val, sem = ("d", d.dma_slot), max(d.dma_val, op.slotmax.get(d.dma_slot, 0)), dsems[d.dma_slot]
                    else:
                        key, val, sem = ("e", d.eng), d.seq, sems[d.eng]
                    if waited.get(key, 0) >= val:
                        continue
                    waited[key] = val
                    eng.wait_ge(sem, val)
                ins = op.fn(eng) if op.fn is not None else None
                if op.is_dma:
                    ins.then_inc(dsems[op.dma_slot], 16)
                elif op.signaled:
                    ins.then_inc(sems[e], 1)

        with nc.Block() as block:
            if streams["pe"]:
                @block.tensor
                def _(eng):
                    run_stream("pe", eng)
            if streams["act"]:
                @block.scalar
                def _(eng):
                    run_stream("act", eng)
            if streams["dve"]:
                @block.vector
                def _(eng):
                    run_stream("dve", eng)
            if streams["pool"]:
                @block.gpsimd
                def _(eng):
                    run_stream("pool", eng)
            if streams["sp"]:
                @block.sync
                def _(eng):
                    run_stream("sp", eng)
        self.streams = {e: [] for e in self.ENG}
        self.phase += 1
        if self.stop_after is not None and self.phase > self.stop_after:
            raise StopBuild()


class Cfg:
    def __init__(self, D=2048, L=2048, LS=1024, TT=512, HH=8, AH=16, KVH=4, DFF=8192, debug=False):
        self.D, self.L, self.LS, self.TT, self.HH, self.AH, self.KVH, self.DFF = D, L, LS, TT, HH, AH, KVH, DFF
        self.KC = D // 128
        self.HGW = HH * 128
        self.AQW = AH * 64
        self.AQC = self.AQW // 128
        self.KVW = KVH * 64
        self.FC = DFF // 128
        self.INW = 4 * self.HGW + self.AQW + 2 * self.KVW + 2 * D
        self.NSEG = L // LS
        self.NT = LS // TT
        self.NSUB = TT // 128
        self.SS = LS // 128
        self.debug = debug
        assert AH // KVH == 4 and self.HH + self.AQC <= self.KC and DFF % (self.KC * 128) == 0
        off = {}
        o = 0
        for name, n in (("c", self.KC), ("bada", 6 * self.KC), ("n1g", self.KC), ("n2g", self.KC), ("lb0", HH),
                        ("lb1", HH), ("hgo", 1), ("gq", 1), ("gk", 1), ("sink", AH // 2), ("ident", 128),
                        ("hgmask", 128), ("bd", 128)):
            off[name] = (o, o + n)
            o += n
        self.ppoff, self.NP = off, o


def t5_bucket(n):
    nb, mx, md = 32, 16, 128
    if n < mx:
        return n
    v = np.float32(np.log(np.float32(n) / np.float32(mx))) / np.float32(math.log(md / mx)) * np.float32(nb - mx)
    return min(int(mx + np.int32(v)), nb - 1)


def host_consts(cfg):
    ohg = np.zeros((33, 512), np.float32)
    for v in range(256):
        if v < 128:
            ohg[t5_bucket(v), v] = 1.0
        else:
            ohg[32, v] = NEG
        if v > 128:
            ohg[t5_bucket(v - 128), 256 + v] = 1.0
        else:
            ohg[32, 256 + v] = NEG
    ident = np.eye(128, dtype=np.float32)
    s = np.arange(128)[:, None]
    c = np.arange(128)[None, :]
    hgmask = ((s // 64 == c // 64) & (s <= c)).astype(np.float32)
    bd = (s // 64 == c // 64).astype(np.float32)
    return ohg, ident, hgmask, bd


def build(cfg):
    nc = bass.Bass("TRN2", target_bir_lowering=False)
    D, L, LS, TT, KC, HH, AH, KVH, DFF, FC = cfg.D, cfg.L, cfg.LS, cfg.TT, cfg.KC, cfg.HH, cfg.AH, cfg.KVH, cfg.DFF, cfg.FC
    HGW, AQW, AQC, KVW, NT, NSUB, SS = cfg.HGW, cfg.AQW, cfg.AQC, cfg.KVW, cfg.NT, cfg.NSUB, cfg.SS
    NPAIR = TT // 128
    NCH = TT // 64

    def din(name, shape):
        return nc.dram_tensor(name, list(shape), F32, kind="ExternalInput").ap()

    x_d = din("x", [L, D])
    pp_d = din("pp", [128, cfg.NP])
    tab_d = din("tabaug", [33, AH])
    ohg_d = din("ohg", [33, 512])
    wada_d = din("w_ada", [D, 6 * D])
    win_d = din("w_in", [D, cfg.INW])
    wbh_d = din("w_bh", [HGW, D])
    wba_d = din("w_ba", [AQW, D])
    wout_d = din("w_out", [D, D])
    wff1_d = din("w_ff1", [D, DFF])
    wff2_d = din("w_ff2", [DFF, D])
    out_d = nc.dram_tensor("out", [L, D], F32, kind="ExternalOutput").ap()
    x1_d = nc.dram_tensor("x1s", [L, D], F32).ap()
    z_d = nc.dram_tensor("zbias", [AH, 2, 130, 256], F32).ap()
    grow_d = nc.dram_tensor("growd", [2, 128, D], F32).ap()
    wff1c_d = nc.dram_tensor("wff1c", [D, DFF], BF16).ap()
    wff2c_d = nc.dram_tensor("wff2c", [DFF, D], BF16).ap()
    dbg = {}
    if cfg.debug:
        dbg["hT"] = nc.dram_tensor("dbg_hT", [128, KC, L], BF16, kind="ExternalOutput").ap()
        dbg["ohg"] = nc.dram_tensor("dbg_ohg", [128, HH, L], BF16, kind="ExternalOutput").ap()
        dbg["oat"] = nc.dram_tensor("dbg_oat", [128, AQC, L], BF16, kind="ExternalOutput").ap()
        dbg["mg"] = nc.dram_tensor("dbg_mg", [128, KC, L], BF16, kind="ExternalOutput").ap()
        dbg["ada"] = nc.dram_tensor("dbg_ada", [128, 6 * KC], F32, kind="ExternalOutput").ap()
        dbg["x1"] = nc.dram_tensor("dbg_x1", [L, D], F32, kind="ExternalOutput").ap()

    HQ0, HF0, HI0, HG0 = 0, HGW, 2 * HGW, 3 * HGW
    AQ0 = 4 * HGW
    AK0 = AQ0 + AQW
    AV0 = AK0 + KVW
    GH0 = AV0 + KVW
    GA0 = GH0 + D

    try:
      with ExitStack() as es:
        S = Sched(nc, es)
        S.stop_after = getattr(cfg, "stop_after", None)
        S.op_limit = getattr(cfg, "op_limit", None)
        cfg._sched = S

        sbn = [0]

        def sb(stack, name, shape, dt):
            sbn[0] += 1
            return stack.enter_context(nc.sbuf_tensor(f"sb{sbn[0]}_{name}", list(shape), dt))

        RW = KC * 512
        ring = []
        ring_i = [0]

        def set_ring(stack, n, width=512):
            ring[:] = [sb(stack, f"ring{i}", [128, KC * width], BF16) for i in range(n)]
            ring_i[0] = 0
        pp = sb(es, "pp", [128, cfg.NP], F32)
        identb = sb(es, "identb", [128, 128], BF16)
        hgmask = sb(es, "hgmaskb", [128, 128], BF16)
        bdb = sb(es, "bdb", [128, 128], BF16)
        onesb = sb(es, "onesb", [128, 128], BF16)
        onesf = sb(es, "onesf", [128, 128], F32)
        ones64 = sb(es, "ones64", [128, 64], F32)
        identf = sb(es, "identf", [128, 128], F32)
        cact = sb(es, "cact", [128, KC], BF16)
        ada = sb(es, "ada", [128, 6 * KC], F32)
        m1s = sb(es, "m1s", [128, KC], F32)
        m2s = sb(es, "m2s", [128, KC], F32)
        lb = sb(es, "lb", [128, HH], F32)
        gq = sb(es, "gq", [128, 1], F32)
        esel = sb(es, "esel", [128, AH // 2], F32)
        hT = sb(es, "hT", [128, KC, LS], BF16)
        Sst = [sb(es, f"Sst{i}", [128, HH, 128], F32) for i in range(2)]
        kcar = sb(es, "kcar", [128, KVH, 128], BF16)
        vcar = sb(es, "vcar", [128, KVH, 64], BF16)
        psf, psb = [], []
        psf_i = [0]
        psb_i = [0]
        psn = [0]
        pools = {}

        def set_psum(stack, nf, nb):
            psn[0] += 1
            psf[:] = [stack.enter_context(nc.psum_tensor(f"psf{psn[0]}_{i}", [128, 512], F32)) for i in range(nf)]
            psb[:] = [stack.enter_context(nc.psum_tensor(f"psb{psn[0]}_{i}", [128, 1024], BF16)) for i in range(nb)]
            psf_i[0] = 0
            psb_i[0] = 0
            pools.clear()

        def bankp(name, idxs):
            c = pools.get(name, 0)
            pools[name] = c + 1
            i = idxs[c % len(idxs)]
            return psf[i], ("psf", i)

        def bank():
            i = psf_i[0] % len(psf)
            psf_i[0] += 1
            return psf[i], ("psf", i)

        def bankb():
            i = psb_i[0] % len(psb)
            psb_i[0] += 1
            return psb[i], ("psb", i)

        def ppc(name):
            a, b = cfg.ppoff[name]
            return pp[:, a:b]

        def act(out, in_, func, reads, writes, bias=None, scale=None, accum=None):
            kw = {}
            if bias is not None:
                kw["bias"] = bias
            if scale is not None:
                kw["scale"] = scale
            if accum is not None:
                kw["accum_out"] = accum
            return S.add("act", lambda e: e.activation(out=out, in_=in_, func=func, **kw), reads=reads, writes=writes)

        def tt(eng, out, in0, in1, op, reads, writes):
            return S.add(eng, lambda e: e.tensor_tensor(out=out, in0=in0, in1=in1, op=op), reads=reads, writes=writes)

        def ts(eng, out, in0, s1, s2, op0, op1, reads, writes):
            if s2 is None:
                return S.add(eng, lambda e: e.tensor_scalar(out=out, in0=in0, scalar1=s1, scalar2=None, op0=op0),
                             reads=reads, writes=writes)
            return S.add(eng, lambda e: e.tensor_scalar(out=out, in0=in0, scalar1=s1, scalar2=s2, op0=op0, op1=op1),
                         reads=reads, writes=writes)

        def stt(out, in0, scalar, in1, op0, op1, reads, writes):
            return S.add("dve", lambda e: e.scalar_tensor_tensor(out=out, in0=in0, scalar=scalar, in1=in1, op0=op0, op1=op1),
                         reads=reads, writes=writes)

        def cp(eng, out, in_, reads, writes):
            if eng == "act":
                return S.add("act", lambda e: e.copy(out=out, in_=in_), reads=reads, writes=writes)
            return S.add(eng, lambda e: e.tensor_copy(out=out, in_=in_), reads=reads, writes=writes)

        def dma(q, out, in_, reads, writes, slot, chain=True):
            return S.add(q, lambda e: e.dma_start(out=out, in_=in_), reads=reads, writes=writes, dma_slot=slot, chain=chain)

        def mmgroup(out, pairs, reads, writes):
            n = len(pairs)

            def fn(e):
                ins = None
                for i, (l, r) in enumerate(pairs):
                    ins = e.matmul(out, lhsT=l, rhs=r, start=(i == 0), stop=(i == n - 1))
                return ins
            return S.add("pe", fn, reads=reads, writes=writes)

        def wload(w_ap, row0, nk, col0, ncols, coloff=0, slot=None, kcoff=0, kstride=None, chain=True):
            if slot is None:
                slot = ring_i[0] % len(ring)
                ring_i[0] += 1
            ks = kstride if kstride is not None else ncols
            view = ring[slot][:, 0:KC * ks].rearrange("p (k n) -> p k n", k=KC)[:, kcoff:kcoff + nk, coloff:coloff + ncols]
            src = w_ap[row0:row0 + nk * 128, col0:col0 + ncols].rearrange("(k p) n -> p k n", p=128)
            if chain:
                dma("pool", view, src, [], [("ring", slot)], f"ring{slot}")
            else:
                op = S.add("pool", lambda e: e.dma_start(out=view, in_=src), dma_slot=f"ring{slot}", chain=False)
                S.last_w[("ring", slot)] = op
            return slot

        cast_jobs = []
        for fct in range(DFF // 512):
            cast_jobs.append((wff1_d, wff1c_d, 0, D, fct * 512, ("wc1", fct)))
        for kb in range(DFF // (KC * 128)):
            for pn in range(D // 512):
                cast_jobs.append((wff2_d, wff2c_d, kb * KC * 128, KC * 128, pn * 512, ("wc2", kb, pn)))
        cast_n = [0]
        NCAST = len(cast_jobs)

        def issue_casts(n):
            for _ in range(n):
                if cast_n[0] >= NCAST:
                    return
                src_t, dst_t, r0, nr, c0, key = cast_jobs[cast_n[0]]
                i = cast_n[0]
                cast_n[0] += 1
                dma("pool", dst_t[r0:r0 + nr, c0:c0 + 512], src_t[r0:r0 + nr, c0:c0 + 512], [], [key], f"cast{i}")

        def wload_c(c_ap, row0, nk, col0, ncols, key):
            slot = ring_i[0] % len(ring)
            ring_i[0] += 1
            view = ring[slot][:, 0:KC * ncols].rearrange("p (k n) -> p k n", k=KC)[:, 0:nk, :]
            src = c_ap[row0:row0 + nk * 128, col0:col0 + ncols].rearrange("(k p) n -> p k n", p=128)
            dma("pool", view, src, [key], [("ring", slot)], f"ring{slot}")
            return slot

        def rview(slot, ks):
            return ring[slot][:, 0:KC * ks].rearrange("p (k n) -> p k n", k=KC)

        def rmsnorm_gen(stack_tmps, par, src_tile, src_key, dstT, col0, msc, msh, dkey, modkey=("mod",)):
            junk, ssq, lnv, rstd, xn = stack_tmps[par]
            act(junk[:], src_tile, AF.Square, [src_key], [("junk", par), ("ssq", par)], accum=ssq[:])
            act(lnv[:], ssq[:], AF.Ln, [("ssq", par)], [("lnv", par)], bias=EPS, scale=1.0 / D)
            act(rstd[:], lnv[:], AF.Exp, [("lnv", par)], [("rstd", par)], scale=-0.5)
            ts("dve", xn[:], src_tile, rstd[:, 0:1], None, ALU.mult, None, [src_key, ("rstd", par)], [("xn", par)])
            yield
            for g in range((KC + 7) // 8):
                n8 = min(8, KC - 8 * g)
                pb, pk = bankb()

                def tr(e, g=g, n8=n8, pb=pb):
                    ins = None
                    for j in range(n8):
                        kc = 8 * g + j
                        ins = e.transpose(out=pb[:, j * 128:(j + 1) * 128], in_=xn[:, kc * 128:(kc + 1) * 128], identity=identb[:])
                    return ins
                S.add("pe", tr, reads=[("xn", par), ("identb",)], writes=[pk])
                for j in range(n8):
                    kc = 8 * g + j
                    if j % 2 == 0:
                        act(dstT[:, kc, col0:col0 + 128], pb[:, j * 128:(j + 1) * 128], AF.Identity, [pk, modkey], [dkey],
                            bias=msh[:, kc:kc + 1], scale=msc[:, kc:kc + 1])
                    else:
                        ts("dve", dstT[:, kc, col0:col0 + 128], pb[:, j * 128:(j + 1) * 128], msc[:, kc:kc + 1], msh[:, kc:kc + 1],
                           ALU.mult, ALU.add, [pk, modkey], [dkey])
                yield

        def norm_tmps(stack):
            return [(sb(stack, f"junk{i}", [128, D], BF16), sb(stack, f"ssq{i}", [128, 1], F32), sb(stack, f"lnv{i}", [128, 1], F32),
                     sb(stack, f"rstd{i}", [128, 1], F32), sb(stack, f"xn{i}", [128, D], BF16)) for i in range(2)]

        def interleave(ga, gb, nb_per_a=1):
            alive_a, alive_b = ga is not None, gb is not None
            while alive_a or alive_b:
                if alive_a:
                    try:
                        next(ga)
                    except StopIteration:
                        alive_a = False
                for _ in range(nb_per_a):
                    if alive_b:
                        try:
                            next(gb)
                        except StopIteration:
                            alive_b = False

        with ExitStack() as ph:
            set_ring(ph, 4)
            set_psum(ph, 6, 2)
            tabs = sb(ph, "tabs", [33, AH], F32)
            ohgs = sb(ph, "ohgs", [33, 512], F32)
            t0 = sb(ph, "t0", [128, 8 * KC], F32)
            t1 = sb(ph, "t1", [128, 8 * KC], F32)
            gsb = sb(ph, "gsb", [AH, 512], F32)

            dma("sp", pp[:], pp_d[:, :], [], [("pp",)], "c0")
            dma("sp", tabs[:], tab_d[:, :], [], [("tabs",)], "c0")
            dma("sp", ohgs[:], ohg_d[:, :], [], [("ohgs",)], "c0")
            cp("dve", identb[:], ppc("ident"), [("pp",)], [("identb",)])
            cp("dve", identf[:], ppc("ident"), [("pp",)], [("identf",)])
            cp("dve", hgmask[:], ppc("hgmask"), [("pp",)], [("hgmask",)])
            cp("dve", bdb[:], ppc("bd"), [("pp",)], [("bdb",)])
            S.add("pool", lambda e: e.memset(onesb[:], 1.0), writes=[("onesb",)])
            S.add("pool", lambda e: e.memset(onesf[:], 1.0), writes=[("onesf",)])
            S.add("pool", lambda e: e.memset(ones64[:], 1.0), writes=[("ones64",)])
            for i in range(2):
                S.add("pool", lambda e, i=i: e.memset(Sst[i][:], 0.0), writes=[("Sst", i)])
            act(t0[:, 0:KC], ppc("c"), AF.Exp, [("pp",)], [("t0",)], scale=-1.0)
            ts("dve", t0[:, 0:KC], t0[:, 0:KC], 1.0, None, ALU.add, None, [("t0",)], [("t0",)])
            S.add("dve", lambda e: e.reciprocal(out=t1[:, 0:KC], in_=t0[:, 0:KC]), reads=[("t0",)], writes=[("t1",)])
            tt("dve", cact[:], t1[:, 0:KC], ppc("c"), ALU.mult, [("t1",), ("pp",)], [("cact",)])
            tt("dve", t0[:, 0:HH], ppc("lb1"), ppc("lb0"), ALU.subtract, [("pp",), ("t1",)], [("t0",)])
            act(t0[:, 0:HH], t0[:, 0:HH], AF.Exp, [("t0",)], [("t0",)])
            ts("dve", t0[:, 0:HH], t0[:, 0:HH], 1.0, None, ALU.add, None, [("t0",)], [("t0",)])
            S.add("dve", lambda e: e.reciprocal(out=lb[:], in_=t0[:, 0:HH]), reads=[("t0",)], writes=[("lb",)])
            ts("dve", gq[:], ppc("gq"), 0.125, None, ALU.mult, None, [("pp",)], [("gq",)])
            act(esel[:], ppc("sink"), AF.Exp, [("pp",)], [("esel",)])
            gps, gk_ = bank()
            mmgroup(gps[0:AH, :], [(tabs[:], ohgs[:])], [("tabs",), ("ohgs",)], [gk_])
            cp("dve", gsb[:], gps[0:AH, :], [gk_], [("gsb",)])
            zops = []
            for cpv in range(2):
                src = gsb[:, cpv * 256:(cpv + 1) * 256].unsqueeze(1).to_broadcast([AH, 130, 256])
                zops.append(dma("sp", z_d[:, cpv, :, :], src, [("gsb",)], [("z", cpv)], "z"))
            def ada_tiles(t0_, t1_, adaps, adak):
                for t in range(t0_, t1_):
                    slot = wload(wada_d, 0, KC, t * 512, 512)
                    wv = rview(slot, 512)
                    for j in range(4):
                        col = t * 4 + j
                        mmgroup(adaps[:, col:col + 1], [(wv[:, kc, j * 128:(j + 1) * 128], cact[:, kc:kc + 1]) for kc in range(KC)],
                                [("ring", slot), ("cact",)], [adak])
            ntile = 6 * D // 512
            nt0 = 2 * D // 512
            adaps, adak = bank()
            ada_tiles(0, nt0, adaps, adak)
            tt("dve", ada[:, 0:2 * KC], adaps[:, 0:2 * KC], ppc("bada")[:, 0:2 * KC], ALU.add, [adak, ("pp",)], [("ada",), ("mod",)])
            stt(m1s[:], ada[:, KC:2 * KC], 1.0, ppc("n1g"), ALU.add, ALU.mult, [("ada",), ("pp",)], [("mod",)])
            S.emit_phase()

        out_stores = []
        for seg in range(cfg.NSEG):
            tok0 = seg * LS
            with ExitStack() as ph:
                set_psum(ph, 2, 4)
                NXS = 3 if seg > 0 else 2
                xs = [sb(ph, f"xs{i}", [128, D], F32) for i in range(NXS)]
                tmps = norm_tmps(ph)
                if seg == 0:
                    set_ring(ph, 7)
                    adaps2, adak2 = psf[0], ("psf", 0)
                    per = (ntile - nt0 + SS - 1) // SS
                gprev1 = None
                for s in range(SS):
                    b = s % 2
                    xb_ = s % NXS
                    r0 = tok0 + s * 128
                    dma("sp", xs[xb_][:], x_d[r0:r0 + 128, :], [], [("xs", xb_)], f"xs{xb_}")
                    g1 = rmsnorm_gen(tmps, b, xs[xb_][:], ("xs", xb_), hT, s * 128, m1s, ada[:, 0:KC], ("hT", s))
                    next(g1)
                    interleave(None, gprev1)
                    gprev1 = g1
                    if seg == 0:
                        ada_tiles(min(ntile, nt0 + s * per), min(ntile, nt0 + (s + 1) * per), adaps2, adak2)
                interleave(None, gprev1)
                if seg == 0:
                    ada_tiles(min(ntile, nt0 + SS * per), ntile, adaps2, adak2)
                    tt("dve", ada[:, 2 * KC:6 * KC], adaps2[:, 2 * KC:6 * KC], ppc("bada")[:, 2 * KC:6 * KC], ALU.add,
                       [adak2, ("pp",)], [("ada2",)])
                    stt(m2s[:], ada[:, 4 * KC:5 * KC], 1.0, ppc("n2g"), ALU.add, ALU.mult, [("ada2",), ("pp",)], [("mod2",)])
                    dgb = sb(ph, "dgb", [128, KC, 128], F32)
                    grow = sb(ph, "grow", [128, D], F32)
                    for gi in (2, 5):
                        tt("dve", dgb[:], identf[:].unsqueeze(1).to_broadcast([128, KC, 128]),
                           ada[:, gi * KC:(gi + 1) * KC].unsqueeze(2).to_broadcast([128, KC, 128]), ALU.mult,
                           [("identf",), ("ada2",)], [("dgb",)])
                        dg2 = dgb[:].rearrange("p k n -> p (k n)")
                        for q_ in range(D // 512):
                            pb_, pk_ = psf[1], ("psf", 1)
                            mmgroup(pb_[:, :], [(onesf[:], dg2[:, q_ * 512:(q_ + 1) * 512])], [("onesf",), ("dgb",)], [pk_])
                            cp("act", grow[:, q_ * 512:(q_ + 1) * 512], pb_[:, :], [pk_], [("grow",)])
                        dma("sp", grow_d[0 if gi == 2 else 1, :, :], grow[:], [("grow",)], [("growd", gi)], "grow")
                if cfg.debug:
                    dma("sp", dbg["hT"][:, :, tok0:tok0 + LS], hT[:], [("hT", s) for s in range(SS)], [], "dbg")
                S.emit_phase()

            with ExitStack() as ph:
              mgT = sb(ph, "mgT", [128, KC, LS], BF16)
              with ExitStack() as pm:
                ohgT = sb(pm, "ohgT", [128, HH, LS], BF16)
                oatT = sb(pm, "oatT", [128, AQC, LS], BF16)
                with ExitStack() as p2:
                    set_ring(p2, 2)
                    set_psum(p2, 7, 1)

                    def f32t(name):
                        return sb(p2, name, [128, TT], F32)
                    on_ = f32t("on_")
                    scc = on_
                    rmask = sb(p2, "rmask", [128, TT], BF16)
                    sqo_ = sb(p2, "sqo_", [128, TT], BF16)
                    AB = []
                    for i in range(3):
                        AB.append(dict(gs=f32t(f"gs{i}"), qt=sb(p2, f"qt{i}", [128, TT], BF16), kt=sb(p2, f"kt{i}", [128, TT], BF16),
                                       qh=sb(p2, f"qh{i}", [128, TT], BF16), kh=sb(p2, f"kh{i}", [128, TT], BF16),
                                       vsb=sb(p2, f"vsb{i}", [128, NPAIR, 128], BF16), dec=sb(p2, f"dec{i}", [128, NCH], F32),
                                       eref=sb(p2, f"eref{i}", [128, NCH], F32), elr=sb(p2, f"elr{i}", [128, NCH], F32)))
                    AO = []
                    for i in range(2):
                        AO.append({n_: f32t(f"{n_}{i}") for n_ in ("u_", "t1_", "lf_", "kv_", "b_", "e1_", "e2_", "q_")})
                        AO[-1]["bm_"] = AO[-1]["t1_"]
                    khtok = [sb(p2, f"khtok{i}", [128, NPAIR, 128], BF16) for i in range(2)]
                    scm = sb(p2, "scm", [128, NPAIR, 128], BF16)
                    Sbf = sb(p2, "Sbf", [128, NCH, 128], BF16)
                    hgo = ppc("hgo")
                    S.add("pool", lambda e: e.memset(rmask[:], 1.0), writes=[("rmask",)])
                    S.add("pool", lambda e: e.memset(rmask[:].rearrange("p (c j) -> p c j", j=64)[:, :, 0:1], 0.0), reads=[("rmask",)], writes=[("rmask",)])
                    for i in range(2):
                        S.add("pool", lambda e, i=i: e.memset(khtok[i][:], 0.0), writes=[("khtok",)])
                    hslot = {}

                    def load_head(h):
                        slot = ring_i[0] % len(ring)
                        ring_i[0] += 1
                        for j, c0 in enumerate((HQ0, HF0, HI0, HG0)):
                            wload(win_d, 0, KC, c0 + h * 128, 128, coloff=j * 128, slot=slot, kstride=512, chain=(j == 0))
                        hslot[h] = slot
                        if seg == 0:
                            issue_casts(1)

                    def stageA(h, t, pb, pa):
                        X = AB[pb]
                        Y = AO[pa]
                        slot = hslot[h]
                        wv = rview(slot, 512)
                        c0t = t * TT
                        hk = [("hT", t * NSUB + s) for s in range(NSUB)]
                        rk = [("ring", slot)]
                        fps, fk = psf[0], ("psf", 0)
                        mmgroup(fps[:, 0:TT], [(wv[:, kc, 128:256], hT[:, kc, c0t:c0t + TT]) for kc in range(KC)], hk + rk, [fk])
                        yield
                        qps, qk = psf[1], ("psf", 1)
                        mmgroup(qps[:, 0:TT], [(wv[:, kc, 0:128], hT[:, kc, c0t:c0t + TT]) for kc in range(KC)], hk + rk, [qk])
                        yield
                        gps_, gk2_ = psf[2], ("psf", 2)
                        mmgroup(gps_[:, 0:TT], [(wv[:, kc, 384:512], hT[:, kc, c0t:c0t + TT]) for kc in range(KC)], hk + rk, [gk2_])
                        yield
                        vps, vk = psf[3], ("psf", 3)
                        for s in range(NSUB):
                            mmgroup(vps[:, s * 128:(s + 1) * 128],
                                    [(hT[:, kc, c0t + s * 128:c0t + (s + 1) * 128], wv[:, kc, 256:384]) for kc in range(KC)],
                                    hk + rk, [vk])
                        yield
                        u_, t1_, lf_, kv_, b_, bm_, e1_, e2_, q_ = [Y[n_] for n_ in ("u_", "t1_", "lf_", "kv_", "b_", "bm_", "e1_", "e2_", "q_")]
                        K_ = lambda n_: (n_, pa)
                        gs_ = X["gs"]
                        act(u_[:], fps[:, 0:TT], AF.Exp, [fk], [K_("u_")], scale=-1.0)
                        act(e1_[:], qps[:, 0:TT], AF.Exp, [qk], [K_("e1_")], scale=-1.0)
                        act(gs_[:], gps_[:, 0:TT], AF.Exp, [gk2_], [("gs_", pb)], scale=-1.0)
                        cp("act", X["vsb"][:], vps[:, 0:TT].rearrange("p (a v) -> p a v", v=128), [vk], [("vsb", pb)])
                        act(e1_[:], e1_[:], AF.Ln, [K_("e1_")], [K_("e1_")], bias=1.0)
                        act(e1_[:], e1_[:], AF.Exp, [K_("e1_")], [K_("e1_")], scale=-1.0)
                        tt("dve", q_[:], qps[:, 0:TT], e1_[:], ALU.mult, [qk, K_("e1_")], [K_("q_")])
                        act(gs_[:], gs_[:], AF.Ln, [("gs_", pb)], [("gs_", pb)], bias=1.0)
                        act(gs_[:], gs_[:], AF.Exp, [("gs_", pb)], [("gs_", pb)], scale=-1.0)
                        tt("dve", gs_[:], gps_[:, 0:TT], gs_[:], ALU.mult, [gk2_, ("gs_", pb)], [("gs_", pb)])
                        act(t1_[:], u_[:], AF.Ln, [K_("u_"), ("lb",)], [K_("t1_")], bias=1.0, scale=lb[:, h:h + 1])
                        act(lf_[:], u_[:], AF.Ln, [K_("u_")], [K_("lf_")], bias=1.0)
                        tt("dve", lf_[:], t1_[:], lf_[:], ALU.subtract, [K_("t1_"), K_("lf_")], [K_("lf_")])
                        S.add("dve", lambda e: e.tensor_tensor_scan(out=b_[:], data0=rmask[:], data1=lf_[:], initial=0.0,
                                                                   op0=ALU.mult, op1=ALU.add), reads=[K_("lf_"), ("rmask",)], writes=[K_("b_")])
                        act(kv_[:], lf_[:], AF.Exp, [K_("lf_")], [K_("kv_")])
                        ts("dve", kv_[:], kv_[:], -1.0, 1.0, ALU.mult, ALU.add, [K_("kv_")], [K_("kv_")])
                        b3 = b_[:].rearrange("p (c j) -> p c j", j=64)
                        bm3 = bm_[:].rearrange("p (c j) -> p c j", j=64)
                        tt("dve", bm3, b3, b3[:, :, 31:32].to_broadcast([128, NCH, 64]), ALU.subtract, [K_("b_")], [K_("t1_")])
                        ts("dve", bm_[:], bm_[:], 80.0, -80.0, ALU.min, ALU.max, [K_("t1_")], [K_("t1_")])
                        act(u_[:], bm_[:], AF.Exp, [K_("t1_"), K_("u_")], [K_("u_")])
                        act(e2_[:], bm_[:], AF.Exp, [K_("t1_")], [K_("e2_")], scale=-1.0)
                        act(X["eref"][:], b3[:, :, 31], AF.Exp, [K_("b_")], [("eref", pb)])
                        tt("dve", X["elr"][:], b3[:, :, 63], b3[:, :, 31], ALU.subtract, [K_("b_")], [("elr", pb)])
                        act(X["elr"][:], X["elr"][:], AF.Exp, [("elr", pb)], [("elr", pb)])
                        act(X["dec"][:], b3[:, :, 63], AF.Exp, [K_("b_")], [("dec", pb)])
                        tt("pool", X["qt"][:], q_[:], u_[:], ALU.mult, [K_("q_"), K_("u_")], [("qt_", pb)])
                        tt("dve", X["kt"][:], kv_[:], e2_[:], ALU.mult, [K_("kv_"), K_("e2_")], [("kt_", pb)])
                        qt3 = X["qt"][:].rearrange("p (c j) -> p c j", j=64)
                        kt3 = X["kt"][:].rearrange("p (c j) -> p c j", j=64)
                        tt("pool", X["qh"][:].rearrange("p (c j) -> p c j", j=64), qt3, X["eref"][:].unsqueeze(2).to_broadcast([128, NCH, 64]),
                           ALU.mult, [("qt_", pb), ("eref", pb)], [("qh_", pb)])
                        tt("dve", X["kh"][:].rearrange("p (c j) -> p c j", j=64), kt3, X["elr"][:].unsqueeze(2).to_broadcast([128, NCH, 64]),
                           ALU.mult, [("kt_", pb), ("elr", pb)], [("kh_", pb)])

                    BB = [4, 5, 6]

                    def stageB(h, t, pb):
                        X = AB[pb]
                        c0t = t * TT
                        qt_, kt_, qh_, kh_, vsb, dec, gs_ = X["qt"], X["kt"], X["qh"], X["kh"], X["vsb"], X["dec"], X["gs"]
                        pb_, pk = bankb()

                        def trk(e, pb_=pb_):
                            ins = None
                            for a in range(NPAIR):
                                ins = e.transpose(out=pb_[:, a * 128:(a + 1) * 128], in_=kh_[:, a * 128:(a + 1) * 128], identity=identb[:])
                            return ins
                        S.add("pe", trk, reads=[("kh_", pb), ("identb",)], writes=[pk])
                        cp("dve", khtok[0][0:64, :, :], pb_[0:64, 0:TT].rearrange("p (a v) -> p a v", v=128), [pk], [("khtok",)])
                        cp("dve", khtok[1][64:128, :, :], pb_[64:128, 0:TT].rearrange("p (a v) -> p a v", v=128), [pk], [("khtok",)])
                        scps, sck = bankp("B", BB)
                        for a in range(NPAIR):
                            mmgroup(scps[:, a * 128:(a + 1) * 128], [(kt_[:, a * 128:(a + 1) * 128], qt_[:, a * 128:(a + 1) * 128])],
                                    [("kt_", pb), ("qt_", pb)], [sck])
                        ts("dve", scc[:], scps[:, 0:TT], 1e30, -1e30, ALU.min, ALU.max, [sck], [("on_",)])
                        tt("dve", scm[:], scc[:].rearrange("p (a v) -> p a v", v=128),
                           hgmask[:].unsqueeze(1).to_broadcast([128, NPAIR, 128]), ALU.mult, [("on_",), ("hgmask",)], [("scm",)])
                        yield
                        for half in range((NCH + 3) // 4):
                            ups, uk = bankp("B", BB)
                            nch = min(4, NCH - 4 * half)
                            for j in range(nch):
                                ch = 4 * half + j
                                a = ch // 2
                                mmgroup(ups[:, j * 128:(j + 1) * 128], [(khtok[ch % 2][:, a, :], vsb[:, a, :])],
                                        [("khtok",), ("vsb", pb)], [uk])
                            for j in range(nch):
                                ch = 4 * half + j
                                gch = t * NCH + ch + seg * (LS // 64)
                                cur, nxt = gch % 2, (gch + 1) % 2
                                cp("pool", Sbf[:, ch, :], Sst[cur][:, h, :], [("Sst", cur, h)], [("Sbf", ch)])
                                stt(Sst[nxt][:, h, :], Sst[cur][:, h, :], dec[:, ch:ch + 1], ups[:, j * 128:(j + 1) * 128],
                                    ALU.mult, ALU.add, [("Sst", cur, h), ("dec", pb), uk], [("Sst", nxt, h)])
                        yield
                        ops_, ok_ = bankp("B", BB)
                        for a in range(NPAIR):
                            mmgroup(ops_[:, a * 128:(a + 1) * 128], [(vsb[:, a, :], scm[:, a, :])], [("vsb", pb), ("scm",)], [ok_])

                            def inter(e, a=a, ops_=ops_):
                                ins = None
                                for r in range(2):
                                    ch = 2 * a + r
                                    ins = e.matmul(ops_[:, ch * 64:(ch + 1) * 64], lhsT=Sbf[:, ch, :], rhs=qh_[:, ch * 64:(ch + 1) * 64],
                                                   start=False, stop=True, skip_group_check=True)
                                return ins
                            S.add("pe", inter, reads=[("Sbf", 2 * a), ("Sbf", 2 * a + 1), ("qh_", pb)], writes=[ok_])
                        act(sqo_[:], ops_[:, 0:TT], AF.Square, [ok_], [("sqo_",)])
                        yield
                        sps, sk = bankp("B", BB)
                        mmgroup(sps[:, 0:TT], [(onesb[:], sqo_[:])], [("onesb",), ("sqo_",)], [sk])
                        act(on_[:], sps[:, 0:TT], AF.Ln, [sk], [("on_",)], bias=EPS, scale=1.0 / 128)
                        act(on_[:], on_[:], AF.Exp, [("on_",)], [("on_",)], scale=-0.5)
                        tt("dve", on_[:], ops_[:, 0:TT], on_[:], ALU.mult, [ok_, ("on_",)], [("on_",)])
                        stt(ohgT[:, h, c0t:c0t + TT], on_[:], hgo[:, 0:1], gs_[:], ALU.mult, ALU.mult,
                            [("on_",), ("gs_", pb), ("pp",)], [("ohg", h, t)])

                    its = [(h, t) for h in range(HH) for t in range(NT)]
                    LAG = 2

                    def adv(g):
                        if g is not None:
                            try:
                                next(g)
                            except StopIteration:
                                pass

                    load_head(0)
                    gprev = None
                    for i, (h, t) in enumerate(its):
                        if t == 0 and h + 1 < HH:
                            load_head(h + 1)
                        j = i - LAG
                        gnew = stageB(its[j][0], its[j][1], j % 3) if j >= 0 else None
                        ga = stageA(h, t, i % 3, i % 2)
                        adv(ga); adv(gprev)
                        adv(ga); adv(gprev); adv(gprev)
                        adv(ga); adv(gnew)
                        adv(ga); adv(gnew)
                        interleave(ga, None)
                        gprev = gnew
                    interleave(None, gprev)
                    for j in range(max(0, len(its) - LAG), len(its)):
                        interleave(None, stageB(its[j][0], its[j][1], j % 3))
                    S.emit_phase()
                with ExitStack() as p3:
                    set_ring(p3, 2)
                    set_psum(p3, 8, 0)
                    NB = LS // 128
                    KH = []
                    for i in range(2):
                        d_ = dict(qTg=[sb(p3, f"qTg{i}_{j}", [128, 2, LS], BF16) for j in range(2)],
                                  kdT=sb(p3, f"kdT{i}", [128, 128 + LS], BF16), vtok=sb(p3, f"vtok{i}", [128, NB + 1, 64], BF16))
                        for j in range(2):
                            S.add("pool", lambda e, q_=d_["qTg"][j]: e.memset(q_[:], 0.0),
                                  writes=[("qTg", i, c_, t_) for c_ in range(2) for t_ in range(NT)])
                        KH.append(d_)
                    for i in range(2):
                        KH[i]["bH"] = sb(p3, f"bH{i}", [128, 2, 4, 128], BF16)
                        KH[i]["bL"] = sb(p3, f"bL{i}", [128, 2, 4, 128], BF16)
                    bstage = sb(p3, "bstage", [128, 2, 4, 128], F32)
                    TA = [dict(sq=sb(p3, f"sq{i}", [128, TT], BF16), r=sb(p3, f"r{i}", [128, TT], F32)) for i in range(2)]
                    TB = [dict(pT=sb(p3, f"pT{i}", [128, 512], BF16)) for i in range(2)]
                    DN = [sb(p3, f"dn{i}", [128, TT], F32) for i in range(2)]
                    gk_c = ppc("gk")
                    kslot = {}
                    cnt3 = {"a": 0, "b": 0, "d": 0}

                    def load_w(kh):
                        slot = ring_i[0] % len(ring)
                        ring_i[0] += 1
                        wload(win_d, 0, KC, AQ0 + kh * 256, 256, coloff=0, slot=slot, kstride=512)
                        wload(win_d, 0, KC, AK0 + kh * 64, 64, coloff=256, slot=slot, kstride=512, chain=False)
                        wload(win_d, 0, KC, AV0 + kh * 64, 64, coloff=320, slot=slot, kstride=512, chain=False)
                        kslot[kh] = slot
                        if seg == 0:
                            issue_casts(2)

                    def load_aux(kh):
                        kp = kh % 2
                        for hl in range(4):
                            hd = 4 * kh + hl
                            for cpv in range(2):
                                src = bass.AP(tensor=z_d.tensor, offset=z_d[hd, cpv, 1, 0].offset, ap=[[255, 128], [1, 128]])
                                dma("sp", bstage[:, hl // 2, (hl % 2) * 2 + (1 - cpv), :], src, [("z", cpv)], [("bstage",)], "bias")
                        bs2 = bstage[:].rearrange("p a b q -> p (a b q)")
                        bh2 = KH[kp]["bH"][:].rearrange("p a b q -> p (a b q)")
                        bl2 = KH[kp]["bL"][:].rearrange("p a b q -> p (a b q)")
                        cp("dve", bh2, bs2, [("bstage",)], [("bH", kp)])
                        tt("dve", bl2, bs2, bh2, ALU.subtract, [("bstage",), ("bH", kp)], [("bL", kp)])
                        if seg > 0:
                            cp("pool", KH[kp]["kdT"][:, 0:128], kcar[:, kh, :], [("kcar", kh)], [("kdT", kp, -1)])
                            cp("pool", KH[kp]["vtok"][:, 0, :], vcar[:, kh, :], [("vcar", kh)], [("vtok", kp, -1)])

                    def stageA3(kh, t):
                        kp = kh % 2
                        Kq, kdT, vtok = KH[kp]["qTg"], KH[kp]["kdT"], KH[kp]["vtok"]
                        wv = rview(kslot[kh], 512)
                        rk = [("ring", kslot[kh])]
                        c0t = t * TT
                        hk = [("hT", t * NSUB + s) for s in range(NSUB)]
                        for c in range(3):
                            ta = cnt3["a"] % 2
                            cnt3["a"] += 1
                            sq_, r_ = TA[ta]["sq"], TA[ta]["r"]
                            pps, ppk = psf[ta], ("psf", ta)
                            if c < 2:
                                mmgroup(pps[:, 0:TT], [(wv[:, kc, c * 128:(c + 1) * 128], hT[:, kc, c0t:c0t + TT]) for kc in range(KC)],
                                        hk + rk, [ppk])
                            else:
                                for r in (0, 64):
                                    mmgroup(pps[r:r + 64, 0:TT], [(wv[:, kc, 256:320], hT[:, kc, c0t:c0t + TT]) for kc in range(KC)],
                                            hk + rk, [ppk])
                            act(sq_[:], pps[:, 0:TT], AF.Square, [ppk], [("sq_", ta)])
                            sps, sk = psf[2], ("psf", 2)
                            mmgroup(sps[:, 0:TT], [(bdb[:], sq_[:])], [("bdb",), ("sq_", ta)], [sk])
                            yield
                            act(r_[:], sps[:, 0:TT], AF.Ln, [sk], [("r_", ta)], bias=EPS, scale=1.0 / 64)
                            act(r_[:], r_[:], AF.Exp, [("r_", ta)], [("r_", ta)], scale=-0.5)
                            if c < 2:
                                for r2 in range(2):
                                    rr = slice(r2 * 64, r2 * 64 + 64)
                                    stt(Kq[r2][rr, c, c0t:c0t + TT], pps[rr, 0:TT], gq[rr, 0:1], r_[rr, :], ALU.mult, ALU.mult,
                                        [ppk, ("r_", ta), ("gq",)], [("qTg", kp, c, t)])
                            else:
                                stt(kdT[:, 128 + c0t:128 + c0t + TT], pps[:, 0:TT], gk_c[:, 0:1], r_[:], ALU.mult, ALU.mult,
                                    [ppk, ("r_", ta), ("pp",)], [("kdT", kp, t)])
                        vps, vk = psf[3], ("psf", 3)
                        for s in range(NSUB):
                            mmgroup(vps[:, s * 64:(s + 1) * 64],
                                    [(hT[:, kc, c0t + s * 128:c0t + (s + 1) * 128], wv[:, kc, 320:384]) for kc in range(KC)], hk + rk, [vk])
                        cp("act", vtok[:, 1 + t * NSUB:1 + (t + 1) * NSUB, :], vps[:, 0:NSUB * 64].rearrange("p (a v) -> p a v", v=64),
                           [vk], [("vtok", kp, t)])
                        yield

                    def stageB3(kh, t):
                        kp = kh % 2
                        Kq, kdT, vtok = KH[kp]["qTg"], KH[kp]["kdT"], KH[kp]["vtok"]
                        c0t = t * TT
                        for pr in range(2):
                            hp = 2 * kh + pr
                            nps, nk_ = psf[4], ("psf", 4)
                            dps, dk_ = psf[5], ("psf", 5)
                            bH4 = KH[kp]["bH"][:, pr, :, :].rearrange("p a q -> p (a q)")
                            bL4 = KH[kp]["bL"][:, pr, :, :].rearrange("p a q -> p (a q)")

                            def logits(bl):
                                n = t * NSUB + bl
                                first = (seg == 0 and n == 0)
                                tb = cnt3["b"] % 2
                                cnt3["b"] += 1
                                pT_ = TB[tb]["pT"]
                                lps, lk = psf[6 + tb], ("psf", 6 + tb)
                                pt_ = t - 1 if (bl == 0 and t > 0) else (-1 if bl == 0 else t)
                                kreads = [("kdT", kp, t), ("kdT", kp, pt_), ("qTg", kp, pr, t), ("bH", kp), ("bL", kp), ("identb",)]
                                for r2 in range(2):
                                    for pc in range(2):
                                        if pc == 0 and first:
                                            continue
                                        kc0 = n * 128 + pc * 128
                                        sl_ = slice((r2 * 2 + pc) * 128, (r2 * 2 + pc + 1) * 128)
                                        mmgroup(lps[:, sl_], [(kdT[:, kc0:kc0 + 128], Kq[r2][:, pr, n * 128:(n + 1) * 128]),
                                                              (identb[:], bH4[:, sl_]), (identb[:], bL4[:, sl_])], kreads, [lk])
                                sl = [slice(0, 512)] if not first else [slice(128, 256), slice(384, 512)]
                                for s_ in sl:
                                    act(pT_[:, s_], lps[:, s_], AF.Exp, [lk], [("pT_", tb)])
                                return (bl, n, first, tb, pt_)

                            def pv(bl, n, first, tb, pt_):
                                pT_ = TB[tb]["pT"]
                                vreads = [("vtok", kp, t), ("vtok", kp, pt_), ("pT_", tb)]
                                for r2 in range(2):
                                    r = r2 * 64
                                    prs = []
                                    drs = []
                                    for pc in range(2):
                                        if pc == 0 and first:
                                            continue
                                        pslice = pT_[:, (r2 * 2 + pc) * 128:(r2 * 2 + pc + 1) * 128]
                                        prs.append((vtok[:, n + pc, :], pslice))
                                        drs.append((onesb[:, 0:64], pslice))
                                    mmgroup(nps[r:r + 64, bl * 128:(bl + 1) * 128], prs, vreads, [nk_])
                                    mmgroup(dps[r:r + 64, bl * 128:(bl + 1) * 128], drs, vreads + [("onesb",)], [dk_])

                            prev = None
                            for bl in range(NSUB):
                                cur = logits(bl)
                                if prev is not None:
                                    pv(*prev)
                                prev = cur
                                yield
                            pv(*prev)
                            di = cnt3["d"] % 2
                            cnt3["d"] += 1
                            dn_ = DN[di]
                            act(dn_[:], dps[:, 0:TT], AF.Ln, [dk_, ("esel",)], [("dn_", di)], bias=esel[:, hp:hp + 1])
                            act(dn_[:], dn_[:], AF.Exp, [("dn_", di)], [("dn_", di)], scale=-1.0)
                            tt("dve", oatT[:, hp, c0t:c0t + TT], nps[:, 0:TT], dn_[:], ALU.mult, [nk_, ("dn_", di)], [("oat", hp, t)])
                            yield
                        if t == NT - 1 and seg + 1 < cfg.NSEG:
                            cp("pool", kcar[:, kh, :], kdT[:, LS:LS + 128], [("kdT", kp, NT - 1)], [("kcar", kh)])
                            cp("pool", vcar[:, kh, :], vtok[:, NB, :], [("vtok", kp, NT - 1)], [("vcar", kh)])

                    its3 = [(kh, t) for kh in range(KVH) for t in range(NT)]
                    load_w(0)
                    load_aux(0)
                    for i, (kh, t) in enumerate(its3):
                        if t == 0 and kh + 1 < KVH:
                            load_w(kh + 1)
                        gb = stageB3(*its3[i - 1]) if i > 0 else None
                        interleave(stageA3(kh, t), gb, nb_per_a=3)
                        if t == 0 and kh + 1 < KVH:
                            load_aux(kh + 1)
                    interleave(None, stageB3(*its3[-1]))
                    S.emit_phase()
                if cfg.debug:
                    dma("sp", dbg["ohg"][:, :, tok0:tok0 + LS], ohgT[:], [("ohg", h, t) for h in range(HH) for t in range(NT)], [], "dbg")
                    dma("sp", dbg["oat"][:, :, tok0:tok0 + LS], oatT[:], [("oat", c, t) for c in range(AQC) for t in range(NT)], [], "dbg")
                with ExitStack() as p4:
                    set_ring(p4, 8, width=256)
                    set_psum(p4, 8, 0)
                    P4T = [dict(s1=sb(p4, f"s1_{i}", [128, TT], F32), s2=sb(p4, f"s2_{i}", [128, TT], F32),
                                m1=sb(p4, f"m1_{i}", [128, TT], F32)) for i in range(2)]
                    it4 = 0
                    CW = 256
                    for ct in range(D // CW):
                        sl_gh = wload(win_d, 0, KC, GH0 + ct * CW, CW)
                        sl_ga = wload(win_d, 0, KC, GA0 + ct * CW, CW)
                        sl_br = ring_i[0] % len(ring)
                        ring_i[0] += 1
                        wload(wbh_d, 0, HH, ct * CW, CW, slot=sl_br, kcoff=0)
                        wload(wba_d, 0, AQC, ct * CW, CW, slot=sl_br, kcoff=HH, chain=False)
                        if seg == 0 and ct % 2 == 1:
                            issue_casts(3)
                        wgh, wga, wbr = rview(sl_gh, CW), rview(sl_ga, CW), rview(sl_br, CW)
                        for j in range(CW // 128):
                            n_ = ct * (CW // 128) + j
                            for t in range(NT):
                                c0t = t * TT
                                pb4 = it4 % 2
                                it4 += 1
                                s1_, s2_, m1_ = P4T[pb4]["s1"], P4T[pb4]["s2"], P4T[pb4]["m1"]
                                hk = [("hT", t * NSUB + s) for s in range(NSUB)]
                                ghps, ghk = bank()
                                mmgroup(ghps[:, 0:TT], [(wgh[:, kc, j * 128:(j + 1) * 128], hT[:, kc, c0t:c0t + TT]) for kc in range(KC)],
                                        hk + [("ring", sl_gh)], [ghk])
                                gaps, gak = bank()
                                mmgroup(gaps[:, 0:TT], [(wga[:, kc, j * 128:(j + 1) * 128], hT[:, kc, c0t:c0t + TT]) for kc in range(KC)],
                                        hk + [("ring", sl_ga)], [gak])
                                act(s1_[:], ghps[:, 0:TT], AF.Exp, [ghk], [("s1_", pb4)], scale=-1.0)
                                act(s2_[:], gaps[:, 0:TT], AF.Exp, [gak], [("s2_", pb4)], scale=-1.0)
                                bhps, bhk = bank()
                                mmgroup(bhps[:, 0:TT], [(wbr[:, k, j * 128:(j + 1) * 128], ohgT[:, k, c0t:c0t + TT]) for k in range(HH)],
                                        [("ohg", k, t) for k in range(HH)] + [("ring", sl_br)], [bhk])
                                baps, bak = bank()
                                mmgroup(baps[:, 0:TT], [(wbr[:, HH + k, j * 128:(j + 1) * 128], oatT[:, k, c0t:c0t + TT]) for k in range(AQC)],
                                        [("oat", k, t) for k in range(AQC)] + [("ring", sl_br)], [bak])
                                for (sx, sxk) in ((s1_, ("s1_", pb4)), (s2_, ("s2_", pb4))):
                                    act(sx[:], sx[:], AF.Ln, [sxk], [sxk], bias=1.0)
                                    act(sx[:], sx[:], AF.Exp, [sxk], [sxk], scale=-1.0)
                                tt("dve", m1_[:], bhps[:, 0:TT], s1_[:], ALU.mult, [bhk, ("s1_", pb4)], [("m1_", pb4)])
                                tt("dve", s2_[:], baps[:, 0:TT], s2_[:], ALU.mult, [bak, ("s2_", pb4)], [("s2_", pb4)])
                                tt("dve", mgT[:, n_, c0t:c0t + TT], m1_[:], s2_[:], ALU.add, [("m1_", pb4), ("s2_", pb4)], [("mg", t)])
                    if cfg.debug:
                        dma("sp", dbg["mg"][:, :, tok0:tok0 + LS], mgT[:], [("mg", t) for t in range(NT)], [], "dbg")
                    S.emit_phase()
              if True:
                with ExitStack() as p5:
                    set_ring(p5, max(4, D // 512))
                    set_psum(p5, 4, 4)
                    xs = [sb(p5, f"xs5_{i}", [128, D], F32) for i in range(2)]
                    x1s = [sb(p5, f"x1s_{i}", [128, D], F32) for i in range(2)]
                    g1row = sb(p5, "g1row", [128, D], F32)
                    dma("sp", g1row[:], grow_d[0, :, :], [("growd", 2)], [("g1row",)], "grow")
                    tmps = norm_tmps(p5)
                    ty = [sb(p5, f"ty{i}", [128, 512], F32) for i in range(2)]
                    wslots = [wload(wout_d, 0, KC, ct * 512, 512) for ct in range(D // 512)]
                    if seg == 0:
                        issue_casts(NCAST)
                    assert len(wslots) <= len(ring)

                    def projP(s):
                        b = s % 2
                        r0 = tok0 + s * 128
                        t = s // NSUB
                        cs = s * 128
                        dma("sp", xs[b][:], x_d[r0:r0 + 128, :], [], [("xs5", b)], f"xs{b}")
                        for ct in range(D // 512):
                            wv = rview(wslots[ct], 512)
                            yps, yk = bank()
                            mmgroup(yps[:, :], [(mgT[:, kc, cs:cs + 128], wv[:, kc, :]) for kc in range(KC)],
                                    [("mg", t), ("ring", wslots[ct])], [yk])
                            tyb = ty[ct % 2]
                            tt("dve", tyb[:], yps[:, :], g1row[:, ct * 512:(ct + 1) * 512], ALU.mult, [yk, ("g1row",)], [("ty", ct % 2)])
                            tt("pool", x1s[b][:, ct * 512:(ct + 1) * 512], tyb[:], xs[b][:, ct * 512:(ct + 1) * 512], ALU.add,
                               [("ty", ct % 2), ("xs5", b)], [("x1s", b)])
                            yield
                        dma("sp", x1_d[r0:r0 + 128, :], x1s[b][:], [("x1s", b)], [("x1d", seg, s)], f"x1st{b}")
                        if cfg.debug:
                            dma("sp", dbg["x1"][r0:r0 + 128, :], x1s[b][:], [("x1s", b)], [], "dbg")

                    def normN(s):
                        b = s % 2
                        return rmsnorm_gen(tmps, b, x1s[b][:], ("x1s", b), hT, s * 128, m2s, ada[:, 3 * KC:4 * KC], ("hT", s), modkey=("mod2",))

                    for s in range(SS):
                        interleave(projP(s), normN(s - 1) if s > 0 else None)
                    interleave(None, normN(SS - 1))
                    S.emit_phase()

            with ExitStack() as p6:
                set_ring(p6, 4)
                set_psum(p6, 8, 0)
                aT = sb(p6, "aT", [128, FC, TT], BF16)
                rl = [sb(p6, f"rl{i}", [128, TT], F32) for i in range(2)]
                xp = [sb(p6, f"xp{i}", [128, 512], F32) for i in range(2)]
                to = sb(p6, "to", [128, 512], F32)
                osb = [sb(p6, f"osb{i}", [128, 512], F32) for i in range(2)]
                g2row = sb(p6, "g2row", [128, D], F32)
                dma("sp", g2row[:], grow_d[1, :, :], [("growd", 5)], [("g2row",)], "grow")
                cnt = 0
                for t in range(NT):
                    c0t = t * TT
                    hk = [("hT", t * NSUB + s) for s in range(NSUB)]
                    for fct in range(DFF // 512):
                        slot = wload_c(wff1c_d, 0, KC, fct * 512, 512, ("wc1", fct))
                        wv = rview(slot, 512)
                        for j in range(4):
                            fc = fct * 4 + j
                            aps, ak_ = bank()
                            mmgroup(aps[:, 0:TT], [(wv[:, kc, j * 128:(j + 1) * 128], hT[:, kc, c0t:c0t + TT]) for kc in range(KC)],
                                    hk + [("ring", slot)], [ak_])
                            rb = fc % 2
                            act(rl[rb][:], aps[:, 0:TT], AF.Relu, [ak_], [("rl", rb)])
                            tt("dve", aT[:, fc, :], rl[rb][:], rl[rb][:], ALU.mult, [("rl", rb)], [("aT", fc // KC)])
                    NKB = FC // KC
                    for pn in range(D // 512):
                        accs = [bank() for _ in range(NSUB)]
                        for kb in range(NKB):
                            slot = wload_c(wff2c_d, kb * KC * 128, KC, pn * 512, 512, ("wc2", kb, pn))
                            wv = rview(slot, 512)
                            for s in range(NSUB):
                                def f2(e, s=s, kb=kb, wv=wv, acc=accs[s][0]):
                                    ins = None
                                    for kc in range(KC):
                                        ins = e.matmul(acc[:, :], lhsT=aT[:, kb * KC + kc, s * 128:(s + 1) * 128], rhs=wv[:, kc, :],
                                                       start=(kb == 0 and kc == 0), stop=(kb == NKB - 1 and kc == KC - 1))
                                    return ins
                                S.add("pe", f2, reads=[("aT", kb), ("ring", slot)], writes=[accs[s][1]])
                        for s in range(NSUB):
                            sg = t * NSUB + s
                            r0 = tok0 + sg * 128
                            b = cnt % 2
                            cnt += 1
                            dma("sp", xp[b][:], x1_d[r0:r0 + 128, pn * 512:(pn + 1) * 512], [("x1d", seg, sg)], [("xp", b)], f"xp{b}")
                            tt("dve", to[:], accs[s][0][:, :], g2row[:, pn * 512:(pn + 1) * 512], ALU.mult, [accs[s][1], ("g2row",)], [("to",)])
                            tt("dve", osb[b][:], to[:], xp[b][:], ALU.add, [("to",), ("xp", b)], [("osb", b)])
                            out_stores.append(dma("sp", out_d[r0:r0 + 128, pn * 512:(pn + 1) * 512], osb[b][:], [("osb", b)], [], f"ost{b}"))
                last = (seg == cfg.NSEG - 1)
                fw = []
                if last:
                    fw = [out_stores[-1], out_stores[-2]]
                    if cfg.debug:
                        fw.append(S.add("sp", lambda e: e.dma_start(out=dbg["ada"][:, :], in_=ada[:]), reads=[], writes=[], dma_slot="dbg"))
                S.emit_phase(final_waits=fw)
    except StopBuild:
        pass
    return nc


def host_pack(cfg, b, c, b_ada, norm1_g, norm2_g, hg_lb_logits, hg_out_norm_g, q_norm_g, k_norm_g, attn_sinks):
    KC, HH, AH = cfg.KC, cfg.HH, cfg.AH
    ohg, ident, hgmask, bd = host_consts(cfg)
    pp = np.zeros((128, cfg.NP), np.float32)

    def put(name, arr):
        a, bb = cfg.ppoff[name]
        pp[:, a:bb] = arr
    put("c", c[b].reshape(KC, 128).T)
    put("bada", b_ada.reshape(6 * KC, 128).T)
    put("n1g", norm1_g.reshape(KC, 128).T)
    put("n2g", norm2_g.reshape(KC, 128).T)
    put("lb0", hg_lb_logits[0].reshape(HH, 128).T)
    put("lb1", hg_lb_logits[1].reshape(HH, 128).T)
    put("hgo", hg_out_norm_g.reshape(128, 1))
    put("gq", np.tile(q_norm_g.reshape(64), 2).reshape(128, 1))
    put("gk", np.tile(k_norm_g.reshape(64), 2).reshape(128, 1))
    sk = attn_sinks.reshape(AH // 2, 2)
    put("sink", np.concatenate([np.repeat(sk[:, 0][None, :], 64, 0), np.repeat(sk[:, 1][None, :], 64, 0)], 0))
    put("ident", ident)
    put("hgmask", hgmask)
    put("bd", bd)
    return pp, ohg


def run(cfg, inputs, runner=None, n_cores=None):
    f = lambda k: np.asarray(inputs[k], dtype=np.float32)
    x, c = f("x"), f("c")
    B = x.shape[0]
    nc = build(cfg)
    tabaug = np.concatenate([f("rel_bias_table"), np.ones((1, cfg.AH), np.float32)], 0)
    shared = {
        "w_ada": np.ascontiguousarray(f("w_ada")[0]), "w_in": np.ascontiguousarray(f("w_in")[0]),
        "w_bh": np.ascontiguousarray(f("w_branch_hg")[0]), "w_ba": np.ascontiguousarray(f("w_branch_attn")[0]),
        "w_out": np.ascontiguousarray(f("w_out")[0]), "w_ff1": np.ascontiguousarray(f("w_ff1")[0]),
        "w_ff2": np.ascontiguousarray(f("w_ff2")[0]), "tabaug": tabaug,
    }
    in_maps = []
    for b in range(B):
        pp, ohg = host_pack(cfg, b, c, f("b_ada")[0], f("norm1_g")[0], f("norm2_g")[0], f("hg_lb_logits"),
                            f("hg_out_norm_g")[0], f("q_norm_g")[0], f("k_norm_g")[0], f("attn_sinks")[0])
        m = dict(shared)
        m.update({"x": np.ascontiguousarray(x[b]), "pp": pp, "ohg": ohg})
        in_maps.append(m)
    if runner is not None:
        return runner(nc, in_maps)
    res = run_bass_kernel_spmd(nc, in_maps, core_ids=list(range(B)))
    return np.stack([np.asarray(r["out"], dtype=np.float32) for r in res.results], 0)


def kernel(**inputs):
    return run(Cfg(), inputs)
```
